# Optimizing a Trainium2 kernel written in Bass

```python
import jax
import jax.numpy as jnp
from jax import lax
import numpy as np

D_MODEL = 2048
BATCH = 16
SEQ = 256
DEPTH = 2
DEC_BATCH = 2
DEC_SEQ = 4096
PAST_LEN = 256

GRID_W = 64
N_EVEN = (DEPTH + 1) // 2
N_ODD = DEPTH // 2
N_MOD = 9
D_FF = 5632
RMS_EPS = 1e-6
CHUNK = 64
MLSTM_HEADS = 4
MLSTM_DQK = 128
MLSTM_DV = 256
MLSTM_CONV_W = 3
GLA_HEADS = 4
GLA_DK = 128
GLA_DV = 256
GLA_RANK = 16
GLA_GATE_NORM = 16.0
MIX_W = MLSTM_HEADS * MLSTM_DV + GLA_HEADS * GLA_DV
AB_SIZES = (2 * MLSTM_HEADS * MLSTM_DQK, MLSTM_HEADS * MLSTM_DV, MLSTM_HEADS * MLSTM_DV, 4 * MLSTM_HEADS,
            GLA_HEADS * GLA_DK, GLA_HEADS * GLA_DK, GLA_HEADS * GLA_DV, GLA_HEADS * GLA_DV, 2 * GLA_RANK)
AB_COLS = sum(AB_SIZES)
HEAD_DIM = 128
N_Q_HEADS = D_MODEL // HEAD_DIM
N_KV_HEADS = 4
Q_PER_KV = N_Q_HEADS // N_KV_HEADS
QKV_COLS = (N_Q_HEADS + 2 * N_KV_HEADS) * HEAD_DIM
ROPE_AXIS = HEAD_DIM // 2
ROPE_THETA = 10000.0
Q_BLOCK = 128

kernel_name = 'hybrid_mlstm_gla_gqa_diffusion_step'

F32 = jnp.float32


def rms_unit(x):
    xf = x.astype(F32)
    return xf * lax.rsqrt(jnp.mean(xf * xf, axis=-1, keepdims=True) + RMS_EPS)


def rmsnorm(x, g):
    return (rms_unit(x) * g.astype(F32)).astype(x.dtype)


def modulation(cvec, w, b):
    m = jax.nn.silu(cvec) @ w + b
    return m.reshape(cvec.shape[0], 1, N_MOD, D_MODEL)


def adaln_in(x, g, mod, j):
    return rmsnorm(x, g) * (1.0 + mod[:, :, 3 * j + 1]) + mod[:, :, 3 * j]


def swiglu(h, wg, wu, wd):
    return (jax.nn.silu(h @ wg) * (h @ wu)) @ wd


def macaron_half(x, g, mod, j, wg, wu, wd):
    return x + 0.5 * mod[:, :, 3 * j + 2] * swiglu(adaln_in(x, g, mod, j), wg, wu, wd)


def centred_depthwise_conv(x, w, b):
    width, ch = w.shape
    y = lax.conv_general_dilated(x, w.astype(x.dtype)[:, None, :], window_strides=(1,),
                                 padding=[(width // 2, width // 2)],
                                 dimension_numbers=('NWC', 'WIO', 'NWC'), feature_group_count=ch)
    return y + b.astype(x.dtype)


def to_chunks(a):
    b, s = a.shape[:2]
    a = a.reshape((b, s // CHUNK, CHUNK) + a.shape[2:])
    return a.transpose((1, 0, 3, 2) + tuple(range(4, a.ndim)))


def from_chunks(o):
    o = o.transpose((1, 0, 3, 2) + tuple(range(4, o.ndim)))
    return o.reshape((o.shape[0], o.shape[1] * o.shape[2]) + o.shape[3:])


def mlstm_scan(q, k, v, ig, lf, c0, n0, m0):
    causal = jnp.tril(jnp.ones((CHUNK, CHUNK), dtype=bool))

    def body(carry, inp):
        cm, nv, m = carry
        qc, kc, vc, ic, fc = inp
        b = jnp.cumsum(fc, axis=-1)
        dmat = jnp.where(causal, b[..., :, None] - b[..., None, :] + ic[..., None, :], -jnp.inf)
        inter = b + m[..., None]
        m_t = jnp.maximum(inter, jnp.max(dmat, axis=-1))
        w_intra = jnp.exp(dmat - m_t[..., None])
        w_inter = jnp.exp(inter - m_t)
        s = jnp.einsum('bhtd,bhsd->bhts', qc, kc) * w_intra
        num = w_inter[..., None] * jnp.einsum('bhtd,bhde->bhte', qc, cm) + jnp.einsum('bhts,bhse->bhte', s, vc)
        den = w_inter * jnp.einsum('bhtd,bhd->bht', qc, nv) + jnp.sum(s, axis=-1)
        h = num / jnp.maximum(jnp.abs(den), jnp.exp(-m_t))[..., None]
        g_end = b[..., -1]
        dec = g_end[..., None] - b + ic
        m_new = jnp.maximum(g_end + m, jnp.max(dec, axis=-1))
        ws = jnp.exp(dec - m_new[..., None])
        wc = jnp.exp(g_end + m - m_new)
        c_new = wc[..., None, None] * cm + jnp.einsum('bhs,bhsd,bhse->bhde', ws, kc, vc)
        n_new = wc[..., None] * nv + jnp.einsum('bhs,bhsd->bhd', ws, kc)
        return (c_new, n_new, m_new), h

    xs = (to_chunks(q), to_chunks(k), to_chunks(v), to_chunks(ig), to_chunks(lf))
    (cf, nf, mf), hs = lax.scan(body, (c0, n0, m0), xs)
    return from_chunks(hs), (cf, nf, mf)


def gla_scan(q, k, v, la, s0):
    causal = jnp.tril(jnp.ones((CHUNK, CHUNK), dtype=bool))

    def body(st, inp):
        qc, kc, vc, ac = inp
        bc = jnp.cumsum(ac, axis=2)
        o_inter = jnp.einsum('bhtd,bhde->bhte', qc * jnp.exp(bc), st)
        diff = bc[:, :, :, None, :] - bc[:, :, None, :, :]
        decay = jnp.exp(jnp.where(causal[:, :, None], diff, -jnp.inf))
        a = jnp.einsum('bhtd,bhsd,bhtsd->bhts', qc, kc, decay)
        o = o_inter + jnp.einsum('bhts,bhse->bhte', a, vc)
        b_end = bc[:, :, -1:, :]
        s_new = jnp.exp(b_end[:, :, 0, :])[..., None] * st + jnp.einsum('bhsd,bhse->bhde', kc * jnp.exp(b_end - bc), vc)
        return s_new, o

    sf, os_ = lax.scan(body, s0, (to_chunks(q), to_chunks(k), to_chunks(v), to_chunks(la)))
    return from_chunks(os_), (sf,)


def bidirectional(scan_fn, shared, per_dir, init):
    outs, finals = [], []
    for d in range(2):
        prep = (lambda a: jnp.flip(a, axis=1)) if d == 1 else (lambda a: a)
        args = [prep(a) for a in shared] + [prep(a[:, :, d]) for a in per_dir]
        out, fin = scan_fn(*args, *[s[:, d] for s in init])
        outs.append(prep(out))
        finals.append(fin)
    stacked = tuple(jnp.stack([f0, f1], axis=1) for f0, f1 in zip(finals[0], finals[1]))
    return outs[0] + outs[1], stacked


def mixer_ab(h, w_in, conv_w, conv_b, b_i, b_f, a_out_g, gk_w, gk_b, b_out_g, w_out, c0, n0, m0, s0):
    bsz, s, _ = h.shape
    proj = (h @ w_in).astype(F32)
    a_qk, a_v, a_o, a_g, b_q, b_k, b_v, b_g, b_lr = jnp.split(proj, np.cumsum(AB_SIZES)[:-1].tolist(), axis=-1)
    hqk = MLSTM_HEADS * MLSTM_DQK
    qk = jax.nn.silu(centred_depthwise_conv(a_qk, conv_w, conv_b))
    mq = qk[..., :hqk].reshape(bsz, s, MLSTM_HEADS, MLSTM_DQK)
    mk = qk[..., hqk:].reshape(bsz, s, MLSTM_HEADS, MLSTM_DQK) * (MLSTM_DQK ** -0.5)
    mv = a_v.reshape(bsz, s, MLSTM_HEADS, MLSTM_DV)
    gates = a_g.reshape(bsz, s, 2, 2, MLSTM_HEADS)
    ig = gates[:, :, :, 0] + b_i.astype(F32)
    lf = jax.nn.log_sigmoid(gates[:, :, :, 1] + b_f.astype(F32))
    hm, (cf, nf, mf) = bidirectional(mlstm_scan, (mq, mk, mv), (ig, lf), (c0, n0, m0))
    ya = rms_unit(hm).reshape(bsz, s, -1) * a_out_g * jax.nn.sigmoid(a_o)
    gq = b_q.reshape(bsz, s, GLA_HEADS, GLA_DK) * (GLA_DK ** -0.5)
    gkk = b_k.reshape(bsz, s, GLA_HEADS, GLA_DK)
    gv = b_v.reshape(bsz, s, GLA_HEADS, GLA_DV)
    lr = b_lr.reshape(bsz, s, 2, GLA_RANK)
    glogit = jnp.einsum('bsjr,jrn->bsjn', lr, gk_w.astype(F32)) + gk_b.astype(F32)
    la = (jax.nn.log_sigmoid(glogit) / GLA_GATE_NORM).reshape(bsz, s, 2, GLA_HEADS, GLA_DK)
    hg, (sf,) = bidirectional(gla_scan, (gq, gkk, gv), (la,), (s0,))
    yb = rms_unit(hg).reshape(bsz, s, -1) * b_out_g * jax.nn.silu(b_g)
    y = jnp.concatenate([ya, yb], axis=-1).astype(h.dtype) @ w_out
    return y, (cf, nf, mf, sf)


def axial_rope_tables(n_tokens):
    rows = n_tokens // GRID_W
    t = jnp.arange(rows * GRID_W)
    row = (t // GRID_W).astype(F32)
    col = (t % GRID_W).astype(F32)
    inv = ROPE_THETA ** (-jnp.arange(0, ROPE_AXIS, 2, dtype=F32) / ROPE_AXIS)
    ang = jnp.stack([row[:, None] * inv, col[:, None] * inv], axis=1)
    return jnp.cos(ang), jnp.sin(ang)


def apply_axial_rope(x, cos, sin):
    half = ROPE_AXIS // 2
    xf = x.astype(F32).reshape(x.shape[:-1] + (2, 2, half))
    shape = (x.shape[1],) + (1,) * (x.ndim - 3) + (2, half)
    cs, sn = cos.reshape(shape), sin.reshape(shape)
    x1, x2 = xf[..., 0, :], xf[..., 1, :]
    out = jnp.stack([x1 * cs - x2 * sn, x1 * sn + x2 * cs], axis=-2)
    return out.reshape(x.shape).astype(x.dtype)


def qk_normed_qkv(h, w_qkv, qg, kg):
    bsz, s, _ = h.shape
    proj = h @ w_qkv
    nq, nk = N_Q_HEADS * HEAD_DIM, N_KV_HEADS * HEAD_DIM
    q = proj[..., :nq].reshape(bsz, s, N_KV_HEADS, Q_PER_KV, HEAD_DIM)
    k = proj[..., nq:nq + nk].reshape(bsz, s, N_KV_HEADS, HEAD_DIM)
    v = proj[..., nq + nk:].reshape(bsz, s, N_KV_HEADS, HEAD_DIM)
    q = (rms_unit(q) * qg.astype(F32)).astype(h.dtype)
    k = (rms_unit(k) * kg.astype(F32)).astype(h.dtype)
    return q, k, v


def blocked_attention(q, k, v):
    bsz, sq, kvh, g, hd = q.shape
    qb = jnp.moveaxis(q.reshape(bsz, sq // Q_BLOCK, Q_BLOCK, kvh, g, hd), 1, 0)

    def one_block(qi):
        sc = jnp.einsum('bqkgd,bskd->bkgqs', qi, k).astype(F32) * (hd ** -0.5)
        p = jax.nn.softmax(sc, axis=-1).astype(v.dtype)
        return jnp.einsum('bkgqs,bskd->bqkgd', p, v)

    o = lax.map(one_block, qb)
    return jnp.moveaxis(o, 0, 1).reshape(bsz, sq, kvh * g * hd)


def attn_context(h, w_qkv, qg, kg, w_o):
    q, k, v = qk_normed_qkv(h, w_qkv, qg, kg)
    return blocked_attention(q, k, v) @ w_o, k, v


def attn_latent(h, ck, cv, w_qkv, qg, kg, w_o, cos, sin):
    q, k, v = qk_normed_qkv(h, w_qkv, qg, kg)
    q = apply_axial_rope(q, cos, sin)
    k = apply_axial_rope(k, cos, sin)
    k_all = jnp.concatenate([ck.astype(k.dtype), k], axis=1)
    v_all = jnp.concatenate([cv.astype(v.dtype), v], axis=1)
    return blocked_attention(q, k_all, v_all) @ w_o


def setup_inputs(seed: int = 0) -> dict:
    key = jax.random.key(seed)
    ks = iter(jax.random.split(key, 40))

    def nrm(shape, scale):
        return jax.random.normal(next(ks), shape, F32) * scale

    D = D_MODEL
    return {
        'x_prompt': nrm((BATCH, SEQ, D), 1.0),
        'x_sample': nrm((DEC_BATCH, DEC_SEQ, D), 1.0),
        'c': nrm((DEC_BATCH, D), 1.0),
        'c_ctx': nrm((D,), 1.0),
        'state_mlstm_C': nrm((DEC_BATCH, N_EVEN, 2, MLSTM_HEADS, MLSTM_DQK, MLSTM_DV), 0.3),
        'state_mlstm_n': nrm((DEC_BATCH, N_EVEN, 2, MLSTM_HEADS, MLSTM_DQK), 0.3),
        'state_mlstm_m': nrm((DEC_BATCH, N_EVEN, 2, MLSTM_HEADS), 1.0),
        'state_gla_S': nrm((DEC_BATCH, N_EVEN, 2, GLA_HEADS, GLA_DK, GLA_DV), 1.0),
        'cache_k': nrm((DEC_BATCH, N_ODD, PAST_LEN, N_KV_HEADS, HEAD_DIM), 1.0),
        'cache_v': nrm((DEC_BATCH, N_ODD, PAST_LEN, N_KV_HEADS, HEAD_DIM), 1.0),
        'w_mod': nrm((DEPTH, D, N_MOD * D), 0.5 * D ** -0.5),
        'b_mod': nrm((DEPTH, N_MOD * D), 0.02),
        'norm_g': 1.0 + nrm((DEPTH, 3, D), 0.02),
        'ffn_w_gate': nrm((DEPTH, 2, D, D_FF), D ** -0.5),
        'ffn_w_up': nrm((DEPTH, 2, D, D_FF), D ** -0.5),
        'ffn_w_down': nrm((DEPTH, 2, D_FF, D), D_FF ** -0.5),
        'w_in_ab': nrm((N_EVEN, D, AB_COLS), D ** -0.5),
        'mlstm_conv_w': nrm((N_EVEN, MLSTM_CONV_W, 2 * MLSTM_HEADS * MLSTM_DQK), MLSTM_CONV_W ** -0.5),
        'mlstm_conv_b': nrm((N_EVEN, 2 * MLSTM_HEADS * MLSTM_DQK), 0.02),
        'mlstm_b_i': nrm((N_EVEN, 2, MLSTM_HEADS), 0.1),
        'mlstm_b_f': 3.0 + nrm((N_EVEN, 2, MLSTM_HEADS), 0.5),
        'mlstm_out_g': 1.0 + nrm((N_EVEN, MLSTM_HEADS * MLSTM_DV), 0.02),
        'gla_w_gk': nrm((N_EVEN, 2, GLA_RANK, GLA_HEADS * GLA_DK), GLA_RANK ** -0.5),
        'gla_b_gk': nrm((N_EVEN, 2, GLA_HEADS * GLA_DK), 0.1),
        'gla_out_g': 1.0 + nrm((N_EVEN, GLA_HEADS * GLA_DV), 0.02),
        'w_out_ab': nrm((N_EVEN, MIX_W, D), MIX_W ** -0.5),
        'w_qkv': nrm((N_ODD, D, QKV_COLS), D ** -0.5),
        'q_norm_g': 1.0 + nrm((N_ODD, HEAD_DIM), 0.02),
        'k_norm_g': 1.0 + nrm((N_ODD, HEAD_DIM), 0.02),
        'w_o': nrm((N_ODD, N_Q_HEADS * HEAD_DIM, D), (N_Q_HEADS * HEAD_DIM) ** -0.5),
    }


def reference(x_prompt, x_sample, c, c_ctx, state_mlstm_C, state_mlstm_n, state_mlstm_m, state_gla_S,
              cache_k, cache_v, w_mod, b_mod, norm_g, ffn_w_gate, ffn_w_up, ffn_w_down, w_in_ab,
              mlstm_conv_w, mlstm_conv_b, mlstm_b_i, mlstm_b_f, mlstm_out_g, gla_w_gk, gla_b_gk, gla_out_g,
              w_out_ab, w_qkv, q_norm_g, k_norm_g, w_o):
    bp = x_prompt.shape[0]
    rope_cos, rope_sin = axial_rope_tables(x_sample.shape[1])
    xp, xs = x_prompt, x_sample
    new_c, new_n, new_m, new_s, new_k, new_v = [], [], [], [], [], []
    for l in range(DEPTH):
        mod_p = modulation(c_ctx[None, :], w_mod[l], b_mod[l])
        mod_s = modulation(c, w_mod[l], b_mod[l])
        xp = macaron_half(xp, norm_g[l, 0], mod_p, 0, ffn_w_gate[l, 0], ffn_w_up[l, 0], ffn_w_down[l, 0])
        xs = macaron_half(xs, norm_g[l, 0], mod_s, 0, ffn_w_gate[l, 0], ffn_w_up[l, 0], ffn_w_down[l, 0])
        hp = adaln_in(xp, norm_g[l, 1], mod_p, 1)
        hs = adaln_in(xs, norm_g[l, 1], mod_s, 1)
        if l % 2 == 0:
            e = l // 2
            wts = (w_in_ab[e], mlstm_conv_w[e], mlstm_conv_b[e], mlstm_b_i[e], mlstm_b_f[e], mlstm_out_g[e],
                   gla_w_gk[e], gla_b_gk[e], gla_out_g[e], w_out_ab[e])
            c0 = jnp.zeros((bp, 2, MLSTM_HEADS, MLSTM_DQK, MLSTM_DV), F32)
            n0 = jnp.zeros((bp, 2, MLSTM_HEADS, MLSTM_DQK), F32)
            m0 = jnp.zeros((bp, 2, MLSTM_HEADS), F32)
            s0 = jnp.zeros((bp, 2, GLA_HEADS, GLA_DK, GLA_DV), F32)
            yp, (cf, nf, mf, sf) = mixer_ab(hp, *wts, c0, n0, m0, s0)
            ys, _ = mixer_ab(hs, *wts, state_mlstm_C[:, e].astype(F32), state_mlstm_n[:, e].astype(F32),
                             state_mlstm_m[:, e].astype(F32), state_gla_S[:, e].astype(F32))
            new_c.append(cf)
            new_n.append(nf)
            new_m.append(mf)
            new_s.append(sf)
        else:
            o = l // 2
            yp, kp, vp = attn_context(hp, w_qkv[o], q_norm_g[o], k_norm_g[o], w_o[o])
            ys = attn_latent(hs, cache_k[:, o], cache_v[:, o], w_qkv[o], q_norm_g[o], k_norm_g[o], w_o[o],
                             rope_cos, rope_sin)
            new_k.append(kp)
            new_v.append(vp)
        xp = xp + mod_p[:, :, 5] * yp
        xs = xs + mod_s[:, :, 5] * ys
        xp = macaron_half(xp, norm_g[l, 2], mod_p, 2, ffn_w_gate[l, 1], ffn_w_up[l, 1], ffn_w_down[l, 1])
        xs = macaron_half(xs, norm_g[l, 2], mod_s, 2, ffn_w_gate[l, 1], ffn_w_up[l, 1], ffn_w_down[l, 1])
    return (xp, xs, jnp.stack(new_c, axis=1), jnp.stack(new_n, axis=1), jnp.stack(new_m, axis=1),
            jnp.stack(new_s, axis=1), jnp.stack(new_k, axis=1), jnp.stack(new_v, axis=1))
```

```python
import contextlib
import numpy as np
import concourse.bass as bass
import concourse.mybir as mybir
from concourse.bass_utils import run_bass_kernel_spmd

F32 = mybir.dt.float32
BF16 = mybir.dt.bfloat16
AF = mybir.ActivationFunctionType
ALU = mybir.AluOpType
AX = mybir.AxisListType

ENGS = ("pe", "dve", "act", "pool", "sp")

D = 2048
KC = 16
DFF = 5632
MC = 44
NMOD = 9
EPS = 1e-6


class Res:
    __slots__ = ("name", "last_w", "readers")

    def __init__(self, name=""):
        self.name = name
        self.last_w = None
        self.readers = []


class Op:
    __slots__ = ("eng", "fn", "deps", "is_dma", "semkey", "ev", "needs_sig", "idx", "name", "group")


class Prog:
    def __init__(self, nc):
        self.nc = nc
        self.ops = {e: [] for e in ENGS}
        self.nops = 0
        self.last_real = {}
        self.last_dma = {}

    def _mk(self, eng, fn, reads, writes, is_dma, semkey, name, group=None):
        op = Op()
        op.group = group
        op.eng = eng
        op.fn = fn
        op.is_dma = is_dma
        op.semkey = semkey
        op.ev = None
        op.needs_sig = False
        op.idx = self.nops
        op.name = name
        self.nops += 1
        deps = []
        for r in reads:
            if r.last_w is not None:
                deps.append(r.last_w)
        for w in writes:
            if w.last_w is not None:
                deps.append(w.last_w)
            deps.extend(w.readers)
        out = []
        seen = set()
        for d in deps:
            if d is op or id(d) in seen:
                continue
            seen.add(id(d))
            if d.eng == eng and eng == "pe" and not d.is_dma and not is_dma:
                continue
            out.append(d)
        op.deps = out
        for r in reads:
            r.readers.append(op)
        for w in writes:
            w.last_w = op
            w.readers = []
        self.ops[eng].append(op)
        self.last_real[eng] = op
        if is_dma:
            self.last_dma[semkey] = op
        return op

    def barrier(self):
        deps = list(self.last_real.values()) + list(self.last_dma.values())
        uniq = []
        seen = set()
        for d in deps:
            if id(d) not in seen:
                seen.add(id(d))
                uniq.append(d)
        for e in ENGS:
            op = Op()
            op.group = None
            op.eng = e
            op.fn = None
            op.is_dma = False
            op.semkey = None
            op.ev = None
            op.needs_sig = False
            op.idx = self.nops
            op.name = "barrier"
            op.deps = list(uniq)
            self.nops += 1
            self.ops[e].append(op)

    def op(self, eng, fn, reads=(), writes=(), name=""):
        return self._mk(eng, fn, reads, writes, False, None, name)

    def dma(self, eng, fn, reads=(), writes=(), semkey=None, name="", group=None):
        assert semkey is not None
        return self._mk(eng, fn, reads, writes, True, semkey, name, group)

    def emit(self, final_wait_eng="sp"):
        nc = self.nc
        all_dma = [o for e in ENGS for o in self.ops[e] if o.is_dma]
        term = Op()
        term.eng = final_wait_eng
        term.fn = None
        term.is_dma = False
        term.semkey = None
        term.ev = None
        term.needs_sig = False
        term.idx = self.nops
        term.name = "term"
        term.group = None
        lastk = {}
        for o in all_dma:
            lastk[o.semkey] = o
        term.deps = list(lastk.values())
        self.ops[final_wait_eng].append(term)
        for e in ENGS:
            for o in self.ops[e]:
                for d in o.deps:
                    d.needs_sig = True
        stack = contextlib.ExitStack()
        esem = {}
        for e in ENGS:
            esem[e] = stack.enter_context(nc.semaphore("s_" + e))
        ksem = {}
        kcnt = {}
        for e in ENGS:
            c = 0
            for o in self.ops[e]:
                if o.is_dma:
                    if o.semkey not in ksem:
                        ksem[o.semkey] = stack.enter_context(nc.semaphore("d_%d" % len(ksem)))
                        kcnt[o.semkey] = 0
                    kcnt[o.semkey] += 16
                    o.ev = (ksem[o.semkey], kcnt[o.semkey])
                elif o.needs_sig:
                    c += 1
                    o.ev = (esem[e], c)
        gmax = {}
        for e in ENGS:
            for o in self.ops[e]:
                if o.is_dma and o.group is not None:
                    kk = (o.semkey, o.group)
                    gmax[kk] = max(gmax.get(kk, 0), o.ev[1])
        for e in ENGS:
            for o in self.ops[e]:
                if o.is_dma and o.group is not None:
                    o.ev = (o.ev[0], gmax[(o.semkey, o.group)])
        self.n_sems = len(ksem) + len(ENGS)
        block = stack.enter_context(nc.Block())

        def run(e, eng):
            waited = {}
            for o in self.ops[e]:
                for d in o.deps:
                    sem, val = d.ev
                    k = id(sem)
                    if waited.get(k, 0) >= val:
                        continue
                    waited[k] = val
                    eng.wait_ge(sem, val)
                if o.fn is None:
                    continue
                ins = o.fn(eng)
                if o.is_dma:
                    ins.then_inc(o.ev[0], 16)
                elif o.needs_sig:
                    ins.then_inc(o.ev[0], 1)

        @block.tensor
        def _(eng):
            run("pe", eng)

        @block.vector
        def _(eng):
            run("dve", eng)

        @block.scalar
        def _(eng):
            run("act", eng)

        @block.gpsimd
        def _(eng):
            run("pool", eng)

        @block.sync
        def _(eng):
            run("sp", eng)

        stack.close()


class Rot:
    def __init__(self, items):
        self.items = items
        self.i = 0

    def next(self):
        it = self.items[self.i % len(self.items)]
        self.i += 1
        return it


class Builder:
    def __init__(self, TB, NBLK, stages="all", dbg=False):
        self.TB = TB
        self.NBLK = NBLK
        self.T = TB * NBLK
        self.NT = TB // 512
        self.stages = stages
        self.dbg = dbg
        self.nc = bass.Bass("TRN2", target_bir_lowering=False)
        self.P = Prog(self.nc)
        self.stack = contextlib.ExitStack()
        self.sem_i = 0

    def sb(self, name, shape, dt):
        return self.stack.enter_context(self.nc.sbuf_tensor(name, shape, dt))

    def ps(self, name, shape, dt=F32):
        return self.stack.enter_context(self.nc.psum_tensor(name, shape, dt))

    def dram_in(self, name, shape, dt=F32):
        return self.nc.dram_tensor(name, list(shape), dt, kind="ExternalInput").ap()

    def dram_out(self, name, shape, dt=F32):
        return self.nc.dram_tensor(name, list(shape), dt, kind="ExternalOutput").ap()

    def dram_tmp(self, name, shape, dt=F32):
        kind = "ExternalOutput" if (self.dbg and dt == F32) else "Internal"
        return self.nc.dram_tensor(name, list(shape), dt, kind=kind).ap()

    def key(self, base):
        self.sem_i += 1
        return "%s_%d" % (base, self.sem_i)

    def declare_io(self):
        T = self.T
        self.xT_in = self.dram_in("xT_in", [D, T])
        self.cT = self.dram_in("cT", [128, KC, 2])
        self.w_mod = self.dram_in("w_mod", [2, D, NMOD * D])
        self.b_modT = self.dram_in("b_modT", [2, 128, NMOD * KC])
        self.norm_gT = self.dram_in("norm_gT", [128, 2 * 3 * KC])
        self.ffn_wg = self.dram_in("ffn_w_gate", [2, 2, D, DFF])
        self.ffn_wu = self.dram_in("ffn_w_up", [2, 2, D, DFF])
        self.ffn_wd = self.dram_in("ffn_w_down", [2, 2, DFF, D])
        self.xT_out = self.dram_out("xT_out", [D, T])

    def alloc_common(self):
        TB = self.TB
        self.xT = self.sb("xT", [128, KC, TB], F32)
        self.rx = [[Res("x%d_%d" % (k, t)) for t in range(self.NT)] for k in range(KC)]
        self.hT = self.sb("hT", [128, KC, TB], BF16)
        self.rh = [[Res("h%d_%d" % (k, t)) for t in range(self.NT)] for k in range(KC)]
        self.ones_bf = self.sb("ones_bf", [128, 128], BF16)
        self.r_ones = Res("ones")
        self.P.op("pool", lambda e: e.memset(self.ones_bf[:], 1.0), writes=[self.r_ones])
        self.modT = self.sb("modT", [128, 2, 2, NMOD * KC], F32)
        self.r_mod = [Res("mod%d" % l) for l in range(2)]
        self.s1 = self.sb("s1", [128, 2, 3, 2, KC], F32)
        self.gt = self.sb("gt", [128, 2, 3, 2, KC], F32)
        self.r_s1 = [Res("s1_%d" % l) for l in range(2)]
        self.normg = self.sb("normg", [128, 2 * 3 * KC], F32)
        self.r_normg = Res("normg")
        self.P.dma("sp", lambda e: e.dma_start(out=self.normg[:], in_=self.norm_gT), writes=[self.r_normg],
                   semkey=self.key("c"))
        self.bmod = self.sb("bmod", [128, 2, NMOD * KC], F32)
        self.r_bmod = Res("bmod")
        self.P.dma("sp", lambda e: e.dma_start(out=self.bmod[:], in_=self.b_modT.rearrange("l p j -> p l j")),
                   writes=[self.r_bmod], semkey=self.key("c"))
        self.psA = [(self.ps("psA%d" % i, [128, 512]), Res("psA%d" % i)) for i in range(4)]
        self.psB = [(self.ps("psB%d" % i, [128, 512]), Res("psB%d" % i)) for i in range(3)]
        self.psM = (self.ps("psM", [128, 512]), Res("psM"))
        self.rotA = Rot(self.psA)
        self.rotB = Rot(self.psB)
        self.aT = self.sb("aT", [128, 2, TB], BF16)
        self.r_aT = [[Res() for t in range(self.NT)] for m in range(2)]
        self.wgu = Rot([(self.sb("wgu%d" % i, [128, KC, 256], BF16), Res("wgu%d" % i), "wgu%d" % i) for i in range(3)])
        self.wd = Rot([(self.sb("wd%d" % i, [128, 2, D], BF16), Res("wd%d" % i), "wd%d" % i) for i in range(2)])
        self.sg = Rot([(self.sb("sg%d" % i, [128, 512], BF16), Res("sg%d" % i)) for i in range(2)])
        self.rstd = self.sb("rstd", [128, 512], F32)
        self.r_rstd = Res("rstd")
        self.sq = Rot([(self.sb("sq%d" % i, [128, 512], BF16), Res("sq%d" % i)) for i in range(2)])
        self.tmpn = Rot([(self.sb("tmpn%d" % i, [128, 512], BF16), Res("tmpn%d" % i)) for i in range(2)])

    def modulation(self, l):
        P = self.P
        if not hasattr(self, 'r_cs'):
            self.cT_sb = self.sb("cT_sb", [128, KC, 2], F32)
            self.cs_bf = self.sb("cs_bf", [128, KC, 2], BF16)
            self.r_cT = Res("cT")
            self.r_cs = Res("cs")
            P.dma("sp", lambda e: e.dma_start(out=self.cT_sb[:], in_=self.cT), writes=[self.r_cT], semkey=self.key("c"))
            P.op("act", lambda e: e.activation(out=self.cs_bf[:], in_=self.cT_sb[:], func=AF.Silu),
                 reads=[self.r_cT], writes=[self.r_cs])
        psm, r_psm = self.psM
        NJ = NMOD * KC
        wv = self.w_mod[l].rearrange("(k p) c -> p k c", p=128)
        for jt in range(NJ // 2):
            wt, r_wt, wkey = self.wgu.next()
            P.dma("pool", lambda e, wt=wt, jt=jt: e.dma_start(out=wt[:], in_=wv[:, :, jt * 256:(jt + 1) * 256]),
                  writes=[r_wt], semkey=wkey)

            def mm(e, wt=wt, jt=jt):
                ins = None
                for jj in range(2):
                    j = jt * 2 + jj
                    for k in range(KC):
                        ins = e.matmul(psm[:, j * 2:j * 2 + 2], wt[:, k, jj * 128:(jj + 1) * 128], self.cs_bf[:, k, :],
                                       start=(k == 0), stop=(k == KC - 1))
                return ins
            P.op("pe", mm, reads=[r_wt, self.r_cs], writes=[r_psm])
        for cv in range(2):
            P.op("dve", lambda e, cv=cv: e.tensor_tensor(
                out=self.modT[:, l, cv, :], in0=psm[:, 0:2 * NJ].rearrange("p (j c) -> p j c", c=2)[:, :, cv],
                in1=self.bmod[:, l, :], op=ALU.add),
                reads=[r_psm, self.r_bmod], writes=[self.r_mod[l]])
        for i in range(3):
            for cv in range(2):
                def f(e, i=i, cv=cv):
                    ins = e.scalar_tensor_tensor(
                        out=self.s1[:, l, i, cv, :], in0=self.modT[:, l, cv, (3 * i + 1) * KC:(3 * i + 2) * KC], scalar=1.0,
                        in1=self.normg[:, (l * 3 + i) * KC:(l * 3 + i + 1) * KC], op0=ALU.add, op1=ALU.mult)
                    return ins
                P.op("dve", f, reads=[self.r_mod[l], self.r_normg], writes=[self.r_s1[l]])

                def g(e, i=i, cv=cv):
                    fac = 1.0 if i == 1 else 0.5
                    return e.tensor_scalar(out=self.gt[:, l, i, cv, :], in0=self.modT[:, l, cv, (3 * i + 2) * KC:(3 * i + 3) * KC],
                                           scalar1=fac, scalar2=None, op0=ALU.mult)
                P.op("dve", g, reads=[self.r_mod[l]], writes=[self.r_s1[l]])

    def shift_ap(self, l, i, cv, k):
        j = (3 * i) * KC + k
        return self.modT[:, l, cv, j:j + 1]

    def cv_of(self, blk, t):
        return 0 if (blk == 0 and t == 0) else 1

    def load_x(self, blk, src):
        P = self.P
        TB = self.TB
        v = src.rearrange("(k p) t -> p k t", p=128)
        g = self.key("g")
        for k in range(KC):
            P.dma("sp", lambda e, k=k: e.dma_start(out=self.xT[:, k, :], in_=v[:, k, blk * TB:(blk + 1) * TB]),
                  writes=self.rx[k], semkey="xload", group=g)

    def store_x(self, blk, dst):
        P = self.P
        TB = self.TB
        v = dst.rearrange("(k p) t -> p k t", p=128)
        g = self.key("g")
        for k in range(KC):
            P.dma("sp", lambda e, k=k: e.dma_start(out=v[:, k, blk * TB:(blk + 1) * TB], in_=self.xT[:, k, :]),
                  reads=self.rx[k], semkey="xstore", group=g)

    def adaln(self, blk, l, i):
        P = self.P
        for t in range(self.NT):
            cv = self.cv_of(blk, t)
            ts = slice(t * 512, (t + 1) * 512)
            pst, r_pst = self.rotB.next()
            for k in range(KC):
                sq, r_sq = self.sq.next()
                P.op("act", lambda e, sq=sq, k=k, ts=ts: e.activation(out=sq[:], in_=self.xT[:, k, ts], func=AF.Square),
                     reads=[self.rx[k][t]], writes=[r_sq])
                P.op("pe", lambda e, sq=sq, k=k, pst=pst: e.matmul(pst[:], self.ones_bf[:], sq[:], start=(k == 0), stop=(k == KC - 1)),
                     reads=[r_sq, self.r_ones], writes=[r_pst])
            P.op("act", lambda e, pst=pst: e.activation(out=self.rstd[:], in_=pst[:], func=AF.Sqrt, scale=1.0 / D, bias=self.eps_ap()),
                 reads=[r_pst, self.r_eps], writes=[self.r_rstd])
            P.op("dve", lambda e: e.reciprocal(out=self.rstd[:], in_=self.rstd[:]), reads=[self.r_rstd], writes=[self.r_rstd])
            for k in range(KC):
                tm, r_tm = self.tmpn.next()
                P.op("dve", lambda e, tm=tm, k=k, ts=ts: e.tensor_tensor(out=tm[:], in0=self.xT[:, k, ts], in1=self.rstd[:], op=ALU.mult),
                     reads=[self.rx[k][t], self.r_rstd], writes=[r_tm])
                P.op("act", lambda e, tm=tm, k=k, ts=ts, cv=cv: e.activation(
                    out=self.hT[:, k, ts], in_=tm[:], func=AF.Identity,
                    scale=self.s1[:, l, i, cv, k:k + 1], bias=self.shift_ap(l, i, cv, k)),
                    reads=[r_tm, self.r_s1[l], self.r_mod[l]], writes=[self.rh[k][t]])

    def eps_ap(self):
        return self.eps_t[:, 0:1]

    def alloc_eps(self):
        self.eps_t = self.sb("eps_t", [128, 1], F32)
        self.r_eps = Res("eps")
        self.P.op("pool", lambda e: e.memset(self.eps_t[:], EPS), writes=[self.r_eps])

    def ffn(self, blk, l, half):
        P = self.P
        i = 0 if half == 0 else 2
        NT = self.NT
        wg_v = self.ffn_wg[l, half].rearrange("(k p) c -> p k c", p=128)
        wu_v = self.ffn_wu[l, half].rearrange("(k p) c -> p k c", p=128)
        wd_v = self.ffn_wd[l, half].rearrange("(m p) n -> p m n", p=128)
        for part in range(MC // 2):
            cs = slice(part * 256, (part + 1) * 256)
            wg, r_wg, kg = self.wgu.next()
            P.dma("pool", lambda e, wg=wg, cs=cs: e.dma_start(out=wg[:], in_=wg_v[:, :, cs]), writes=[r_wg], semkey=kg)
            wu, r_wu, ku = self.wgu.next()
            P.dma("pool", lambda e, wu=wu, cs=cs: e.dma_start(out=wu[:], in_=wu_v[:, :, cs]), writes=[r_wu], semkey=ku)
            wd, r_wd, kd = self.wd.next()
            P.dma("pool", lambda e, wd=wd, part=part: e.dma_start(out=wd[:], in_=wd_v[:, part * 2:part * 2 + 2, :]),
                  writes=[r_wd], semkey=kd)
            for m in range(2):
                for t in range(NT):
                    ts = slice(t * 512, (t + 1) * 512)
                    pg, r_pg = self.rotA.next()
                    pu, r_pu = self.rotA.next()

                    def mm(e, w=wg, pp=pg, m=m, ts=ts):
                        ins = None
                        for k in range(KC):
                            ins = e.matmul(pp[:], w[:, k, m * 128:(m + 1) * 128], self.hT[:, k, ts], start=(k == 0), stop=(k == KC - 1))
                        return ins
                    P.op("pe", mm, reads=[r_wg] + [self.rh[k][t] for k in range(KC)], writes=[r_pg])
                    P.op("pe", lambda e, mm=mm, wu=wu, pu=pu: mm(e, w=wu, pp=pu), reads=[r_wu] + [self.rh[k][t] for k in range(KC)], writes=[r_pu])
                    sg, r_sg = self.sg.next()
                    P.op("act", lambda e, sg=sg, pg=pg: e.activation(out=sg[:], in_=pg[:], func=AF.Silu), reads=[r_pg], writes=[r_sg])
                    P.op("dve", lambda e, sg=sg, pu=pu, m=m, ts=ts: e.tensor_tensor(out=self.aT[:, m, ts], in0=sg[:], in1=pu[:], op=ALU.mult),
                         reads=[r_sg, r_pu], writes=[self.r_aT[m][t]])
            for n in range(KC):
                for t in range(NT):
                    cv = self.cv_of(blk, t)
                    ts = slice(t * 512, (t + 1) * 512)
                    py, r_py = self.rotB.next()

                    def mmd(e, py=py, n=n, ts=ts, wd=wd):
                        ins = None
                        for m in range(2):
                            ins = e.matmul(py[:], wd[:, m, n * 128:(n + 1) * 128], self.aT[:, m, ts], start=(m == 0), stop=(m == 1))
                        return ins
                    P.op("pe", mmd, reads=[r_wd, self.r_aT[0][t], self.r_aT[1][t]], writes=[r_py])
                    P.op("dve", lambda e, py=py, n=n, ts=ts, cv=cv: e.scalar_tensor_tensor(
                        out=self.xT[:, n, ts], in0=py[:], scalar=self.gt[:, l, i, cv, n:n + 1], in1=self.xT[:, n, ts],
                        op0=ALU.mult, op1=ALU.add),
                        reads=[r_py, self.r_s1[l], self.rx[n][t]], writes=[self.rx[n][t]])

    def arena_reset(self):
        self.a32 = self.xT[:].rearrange("p a b -> p (a b)")
        self.a16 = self.hT[:].rearrange("p a b -> p (a b)")
        self.a32_off = 0
        self.a16_off = 0

    def t32(self, n, shape=None, parts=128):
        n4 = (n + 3) // 4 * 4
        v = self.a32[0:parts, self.a32_off:self.a32_off + n]
        self.a32_off += n4
        assert self.a32_off <= 24576, self.a32_off
        return v, Res()

    def t16(self, n, parts=128):
        n4 = (n + 7) // 8 * 8
        v = self.a16[0:parts, self.a16_off:self.a16_off + n]
        self.a16_off += n4
        assert self.a16_off <= 24576, self.a16_off
        return v, Res()

    def declare_io2(self):
        T = self.T
        self.w_in = self.dram_in("w_in_ab", [1, D, 6192])
        self.w_out = self.dram_in("w_out_ab", [1, D, D])
        self.w_qkv = self.dram_in("w_qkv", [1, D, 3072])
        self.w_o = self.dram_in("w_o", [1, D, D])
        self.consts = self.dram_in("consts", [128, 768])
        self.convT = self.dram_in("convT", [128, 4, 8])
        self.mbias = self.dram_in("mbias", [4, 4])
        self.outgT = self.dram_in("outgT", [128, 16])
        self.gkwb = self.dram_in("gkwb", [17, 2, 512])
        self.qkg = self.dram_in("qkg", [128, 2])
        self.qkg_row = self.dram_in("qkg_row", [2, 128])
        self.st_C = self.dram_in("st_C", [2, 4, 128, 256])
        self.st_n = self.dram_in("st_n", [2, 4, 128])
        self.st_m = self.dram_in("st_m", [2, 4])
        self.st_S = self.dram_in("st_S", [2, 4, 128, 256])
        self.cacheKT = self.dram_in("cacheKT", [4, 128, 256])
        self.cacheV = self.dram_in("cacheV", [256, 4, 128])
        self.ropeT = self.dram_in("ropeT", [2, 128, 4096])
        self.o_C = self.dram_out("o_C", [2, 2, 4, 128, 256])
        self.o_n = self.dram_out("o_n", [2, 2, 4, 128])
        self.o_m = self.dram_out("o_m", [2, 2, 4])
        self.o_S = self.dram_out("o_S", [2, 2, 4, 128, 256])
        self.o_k = self.dram_out("o_k", [512, 512])
        self.o_v = self.dram_out("o_v", [512, 512])
        self.XS = self.dram_tmp("XS", [D, T])
        self.PF = self.dram_tmp("PF", [16, 128, T])
        self.PT = self.dram_tmp("PT", [T, 4096])
        self.GRa = self.dram_tmp("GRa", [4, 4, T])
        self.GRb = self.dram_tmp("GRb", [2, 16, T])
        self.HMD = self.dram_tmp("HMD", [T, 8, 256])
        self.YT = self.dram_tmp("YT", [16, 128, T], BF16)
        self.QT = self.dram_tmp("QT", [16, 128, T], BF16)
        self.KT = self.dram_tmp("KT", [4, 128, T], BF16)
        self.VT = self.dram_tmp("VT", [T, 512], BF16)
        self.OT = self.dram_tmp("OT", [16, 128, T], BF16)

    def alloc_consts(self):
        P = self.P
        self.cst = self.sb("cst", [128, 768], F32)
        self.r_cst = Res("cst")
        P.dma("sp", lambda e: e.dma_start(out=self.cst[:], in_=self.consts), writes=[self.r_cst], semkey=self.key("c"))
        self.identF = self.cst[:, 0:128]
        self.maskF = [self.cst[:, 128:256], self.cst[:, 256:384]]
        self.onesF = self.cst[:, 384:512]
        self.ropeR = self.cst[:, 512:640]
        self.identB = self.sb("identB", [128, 128], BF16)
        self.r_identB = Res("identB")
        P.op("dve", lambda e: e.tensor_copy(self.identB[:], self.identF), reads=[self.r_cst], writes=[self.r_identB])
        self.sm = self.sb("smallc", [128, 64], F32)
        self.r_sm = Res("sm")
        P.dma("sp", lambda e: e.dma_start(out=self.sm[:, 0:32], in_=self.convT.rearrange("p a b -> p (a b)")), writes=[self.r_sm], semkey=self.key("c"))
        P.dma("sp", lambda e: e.dma_start(out=self.sm[:, 32:48], in_=self.outgT), writes=[self.r_sm], semkey=self.key("c"))
        P.dma("sp", lambda e: e.dma_start(out=self.sm[:, 48:50], in_=self.qkg), writes=[self.r_sm], semkey=self.key("c"))
        self.mb = self.sb("mb", [4, 8], F32)
        self.r_mb = Res("mb")
        P.dma("sp", lambda e: e.dma_start(out=self.mb[:, 0:4], in_=self.mbias), writes=[self.r_mb], semkey=self.key("c"))
        P.op("dve", lambda e: e.tensor_scalar(out=self.mb[:, 4:8], in0=self.mb[:, 0:4], scalar1=-1.0, scalar2=None, op0=ALU.mult),
             reads=[self.r_mb], writes=[self.r_mb])

    def convw(self, tap, idx):
        return self.sm[:, tap * 8 + idx:tap * 8 + idx + 1]

    def stage_tiles(self):
        fl = self.aT[:].rearrange("p a b -> p (a b)").bitcast(F32)
        cells = [self.r_aT[m][t] for m in range(2) for t in range(self.NT)]
        out = []
        for j in range(self.NT):
            out.append((fl[:, j * 512:(j + 1) * 512], [cells[2 * j], cells[2 * j + 1]], "stg%d" % j))
        return Rot(out)

    def proj_fm(self, wv, c0, ncols, dst_fn, post=None):
        P = self.P
        nch = ncols // 128
        for c in range(0, nch, 2):
            w = min(2, nch - c)
            wt, r_wt, wkey = self.wgu.next()
            P.dma("pool", lambda e, wt=wt, c=c, w=w: e.dma_start(out=wt[:, :, 0:w * 128], in_=wv[:, :, c0 + c * 128:c0 + (c + w) * 128]),
                  writes=[r_wt], semkey=wkey)
            for cc in range(w):
                for t in range(self.NT):
                    ts = slice(t * 512, (t + 1) * 512)
                    ps, r_ps = self.rotA.next()

                    def mm(e, wt=wt, cc=cc, ts=ts, ps=ps):
                        ins = None
                        for k in range(KC):
                            ins = e.matmul(ps[:], wt[:, k, cc * 128:(cc + 1) * 128], self.hT[:, k, ts], start=(k == 0), stop=(k == KC - 1))
                        return ins
                    P.op("pe", mm, reads=[r_wt] + [self.rh[k][t] for k in range(KC)], writes=[r_ps])
                    dst_fn(c + cc, t, ps, r_ps)

    def proj_tm(self, wv, c0, dst_fn, tgs=None):
        P = self.P
        wa, r_wa, ka = self.wgu.next()
        P.dma("pool", lambda e: e.dma_start(out=wa[:], in_=wv[:, :, c0:c0 + 256]), writes=[r_wa], semkey=ka)
        wb, r_wb, kb = self.wgu.next()
        P.dma("pool", lambda e: e.dma_start(out=wb[:], in_=wv[:, :, c0 + 256:c0 + 512]), writes=[r_wb], semkey=kb)
        for tg in (tgs if tgs is not None else range(self.TB // 128)):
            t = tg // 4
            ps, r_ps = self.rotA.next()

            def mm(e, tg=tg, ps=ps):
                ins = None
                for (w, off) in ((wa, 0), (wb, 256)):
                    for k in range(KC):
                        ins = e.matmul(ps[:, off:off + 256], self.hT[:, k, tg * 128:(tg + 1) * 128], w[:, k, :], start=(k == 0), stop=(k == KC - 1))
                return ins
            P.op("pe", mm, reads=[r_wa, r_wb] + [self.rh[k][t] for k in range(KC)], writes=[r_ps])
            dst_fn(tg, ps, r_ps)

    def inproj(self, blk):
        P = self.P
        tok0 = blk * self.TB
        wv = self.w_in[0].rearrange("(k p) c -> p k c", p=128)
        stg = self.stage_tiles()
        for gi, c0 in enumerate((0, 512, 3088, 3600)):
            def dst(ci, t, ps, r_ps, gi=gi):
                st, r_st, kst = stg.next()
                P.op("act", lambda e: e.activation(out=st, in_=ps[:], func=AF.Copy), reads=[r_ps], writes=r_st)
                idx = gi * 4 + ci
                P.dma("sp", lambda e: e.dma_start(out=self.PF[idx, :, tok0 + t * 512:tok0 + (t + 1) * 512], in_=st), reads=r_st, semkey=kst)
            self.proj_fm(wv, c0, 512, dst)
        for gi, cs in enumerate((1024, 2048, 4112, 5136)):
            for half in range(2):
                def dst(tg, ps, r_ps, gi=gi, half=half):
                    st, r_st, kst = stg.next()
                    P.op("act", lambda e: e.activation(out=st, in_=ps[:], func=AF.Copy), reads=[r_ps], writes=r_st)
                    P.dma("sp", lambda e: e.dma_start(
                        out=self.PT[tok0 + tg * 128:tok0 + (tg + 1) * 128, gi * 1024 + half * 512:gi * 1024 + (half + 1) * 512], in_=st),
                        reads=r_st, semkey=kst)
                self.proj_tm(wv, cs + half * 512, dst)
        wt, r_wt, wkey = self.wgu.next()
        P.dma("pool", lambda e: e.dma_start(out=wt[:, :, 0:16], in_=wv[:, :, 3072:3088]), writes=[r_wt], semkey=wkey)
        P.dma("pool", lambda e: e.dma_start(out=wt[:, :, 16:48], in_=wv[:, :, 6160:6192]), writes=[r_wt], semkey=wkey)
        groups = [(4 * g, 4, self.GRa[g]) for g in range(4)] + [(16 + 16 * j, 16, self.GRb[j]) for j in range(2)]
        for (co, n, dstap) in groups:
            for t in range(self.NT):
                ts = slice(t * 512, (t + 1) * 512)
                ps, r_ps = self.rotB.next()

                def mm(e, co=co, n=n, ts=ts, ps=ps):
                    ins = None
                    for k in range(KC):
                        ins = e.matmul(ps[0:n, :], wt[:, k, co:co + n], self.hT[:, k, ts], start=(k == 0), stop=(k == KC - 1))
                    return ins
                P.op("pe", mm, reads=[r_wt] + [self.rh[k][t] for k in range(KC)], writes=[r_ps])
                st, r_st, kst = stg.next()
                P.op("act", lambda e, st=st, ps=ps, n=n: e.activation(out=st[0:n, :], in_=ps[0:n, :], func=AF.Copy), reads=[r_ps], writes=r_st)
                P.dma("sp", lambda e, st=st, n=n, dstap=dstap, t=t: e.dma_start(out=dstap[:, tok0 + t * 512:tok0 + (t + 1) * 512], in_=st[0:n, :]),
                      reads=r_st, semkey=kst)

    def mixer_core(self, segs):
        P = self.P
        P.barrier()
        self.arena_reset()
        SM = max(s[1] for s in segs)
        NCM = SM // 128
        R0, r_R0 = self.t32(SM, parts=4)
        R1, r_R1 = self.t32(SM, parts=4)
        R2, r_R2 = self.t32(SM, parts=4)
        RST, r_RST = self.t32(SM, parts=4)
        P.op("pool", lambda e: e.memset(RST, 1.0), writes=[r_RST])
        P.op("pool", lambda e: e.memset(RST.rearrange("p (c l) -> p c l", l=128)[:, :, 0:1], 0.0), writes=[r_RST])
        xin, r_xin = self.t32(1032)
        acc, r_acc = self.t32(1024)
        COLS = [self.t32(NCM * 12) for _ in range(2)]
        Cst, r_Cst = self.t32(260)
        Sst, r_Sst = self.t32(256)
        hrot = Rot([self.t32(256) for _ in range(3)])
        grot = Rot([self.t32(256) for _ in range(2)])
        f128 = Rot([self.t32(128) for _ in range(8)])
        lrt = Rot([self.t32(128, parts=17) for _ in range(2)])
        for (v, r) in lrt.items:
            P.op("pool", lambda e, v=v: e.memset(v, 1.0), writes=[r])
        gk, r_gk = self.t32(1024, parts=17)
        P.dma("sp", lambda e: e.dma_start(out=gk, in_=self.gkwb.rearrange("r d c -> r (d c)")), writes=[r_gk], semkey=self.key("c"))
        AM, r_AM = self.t32(NCM, parts=4)
        MS, r_MS = self.t32(NCM, parts=4)
        WI, r_WI = self.t32(NCM, parts=4)
        mcur, r_mcur = self.t32(4, parts=4)
        wdg, r_wdg = self.t32(4, parts=4)
        small = Rot([self.t32(8) for _ in range(4)])
        QC, r_QC = self.t16(SM)
        KCc, r_KC = self.t16(SM)
        Ktok, r_Ktok = self.t16(NCM * 128)
        Vx, r_Vx = self.t16(NCM * 257)
        b128 = Rot([self.t16(128) for _ in range(8)])
        b260 = Rot([self.t16(260) for _ in range(4)])
        b256 = Rot([self.t16(256) for _ in range(4)])
        pss = Rot(self.psA + self.psB + [self.psM])
        hkey = Rot(["hmd%d" % i for i in range(3)])
        ykey = Rot(["yt%d" % i for i in range(4)])
        r_HMD = {}
        P.barrier()

        def seg_body(seg0, S, kind):
            NC = S // 128
            is_s = kind[0] == 's'
            mfin = [None, None]
            def prep_body(d):
                CL, r_CL = COLS[d]
                r0 = R0[:, 0:S]
                r1 = R1[:, 0:S]
                r2 = R2[:, 0:S]
                P.dma("sp", lambda e, r0=r0, d=d: e.dma_start(out=r0, in_=self.GRa[2 * d][:, seg0:seg0 + S]), writes=[r_R0], semkey="gr0")
                P.dma("sp", lambda e, r1=r1, d=d: e.dma_start(out=r1, in_=self.GRa[2 * d + 1][:, seg0:seg0 + S]), writes=[r_R1], semkey="gr1")
                P.op("dve", lambda e, r0=r0, d=d: e.tensor_scalar(out=r0, in0=r0, scalar1=self.mb[:, 2 * d:2 * d + 1], scalar2=None, op0=ALU.add),
                     reads=[r_R0, self.r_mb], writes=[r_R0])
                P.op("act", lambda e, r1=r1, d=d: e.activation(out=r1, in_=r1, func=AF.Exp, scale=-1.0, bias=self.mb[:, 4 + 2 * d + 1:4 + 2 * d + 2]),
                     reads=[r_R1, self.r_mb], writes=[r_R1])
                P.op("act", lambda e, r1=r1: e.activation(out=r1, in_=r1, func=AF.Ln, scale=1.0, bias=self.onesF[0:4, 0:1]),
                     reads=[r_R1, self.r_cst], writes=[r_R1])
                P.op("dve", lambda e, r1=r1, r2=r2: e.tensor_tensor_scan(out=r2, data0=RST[:, 0:S], data1=r1, initial=0.0, op0=ALU.mult, op1=ALU.add),
                     reads=[r_R1, r_RST], writes=[r_R2])
                if d == 0:
                    NB, r_NB = r2, r_R2
                else:
                    r13 = r1.rearrange("p (c l) -> p c l", l=128)
                    r23 = r2.rearrange("p (c l) -> p c l", l=128)
                    P.op("dve", lambda e, r1=r1, r2=r2: e.tensor_tensor(out=r1, in0=r1, in1=r2, op=ALU.subtract), reads=[r_R1, r_R2], writes=[r_R1])
                    P.op("dve", lambda e, r13=r13, r23=r23, NC=NC: e.tensor_tensor(out=r13, in0=r13, in1=self.bc_last(r23[:, :, 127:128], 128), op=ALU.add),
                         reads=[r_R1, r_R2], writes=[r_R1])
                    NB, r_NB = r1, r_R1
                NB3 = NB.rearrange("p (c l) -> p c l", l=128)
                r03 = r0.rearrange("p (c l) -> p c l", l=128)
                P.op("dve", lambda e, r0=r0, NB=NB: e.tensor_tensor(out=r0, in0=r0, in1=NB, op=ALU.add), reads=[r_R0, r_NB], writes=[r_R0])
                P.op("dve", lambda e, r03=r03, NC=NC: e.tensor_reduce(out=AM[:, 0:NC], in_=r03, axis=AX.X, op=ALU.max), reads=[r_R0], writes=[r_AM])
                if is_s:
                    P.dma("sp", lambda e, d=d: e.dma_start(out=mcur[:, 0:1], in_=self.st_m[d].rearrange("(h o) -> h o", o=1)), writes=[r_mcur], semkey="mc")
                else:
                    P.op("dve", lambda e: e.memset(mcur[:, 0:1], 0.0), writes=[r_mcur])
                order = list(range(NC)) if d == 0 else list(range(NC - 1, -1, -1))
                endcol = 127 if d == 0 else 0
                for c in order:
                    P.op("dve", lambda e, c=c: e.tensor_tensor(out=MS[:, c:c + 1], in0=mcur[:, 0:1], in1=AM[:, c:c + 1], op=ALU.max),
                         reads=[r_mcur, r_AM], writes=[r_MS])
                    P.op("dve", lambda e, c=c: e.tensor_tensor(out=WI[:, c:c + 1], in0=mcur[:, 0:1], in1=MS[:, c:c + 1], op=ALU.subtract),
                         reads=[r_mcur, r_MS], writes=[r_WI])
                    P.op("dve", lambda e, c=c, NB3=NB3, endcol=endcol: e.tensor_tensor(out=mcur[:, 0:1], in0=MS[:, c:c + 1], in1=NB3[:, c, endcol:endcol + 1], op=ALU.subtract),
                         reads=[r_MS, r_NB], writes=[r_mcur])
                if not is_s:
                    P.dma("sp", lambda e, d=d, q=kind[1]: e.dma_start(out=self.o_m[q, d].rearrange("(h o) -> h o", o=1), in_=mcur[:, 0:1]),
                          reads=[r_mcur], semkey="om")
                P.op("act", lambda e, NC=NC: e.activation(out=WI[:, 0:NC], in_=WI[:, 0:NC], func=AF.Exp), reads=[r_WI], writes=[r_WI])
                msb = self.bc_last(MS[:, 0:NC].rearrange("p (c o) -> p c o", o=1), 128)
                P.op("dve", lambda e, r03=r03, msb=msb: e.tensor_tensor(out=r03, in0=r03, in1=msb, op=ALU.subtract), reads=[r_R0, r_MS], writes=[r_R0])
                P.op("act", lambda e, r0=r0: e.activation(out=r0, in_=r0, func=AF.Exp), reads=[r_R0], writes=[r_R0])
                P.op("dve", lambda e, NB3=NB3, msb=msb: e.tensor_tensor(out=NB3, in0=NB3, in1=msb, op=ALU.subtract), reads=[r_NB, r_MS], writes=[r_NB])
                P.op("act", lambda e, NB=NB: e.activation(out=NB, in_=NB, func=AF.Exp), reads=[r_NB], writes=[r_NB])
                for c in range(NC):
                    ps, r_ps = pss.next()
                    P.op("dve", lambda e, c=c: e.tensor_scalar(out=wdg[:, 0:4], in0=self.identF[0:4, 0:4], scalar1=WI[:, c:c + 1], scalar2=None, op0=ALU.mult),
                         reads=[r_WI, self.r_cst], writes=[r_wdg])

                    def pe3(e, c=c, ps=ps, r0=r0, NB=NB):
                        e.transpose(ps[:, 0:4], r0[:, c * 128:(c + 1) * 128], self.identF[0:4, 0:4])
                        e.transpose(ps[:, 4:8], NB[:, c * 128:(c + 1) * 128], self.identF[0:4, 0:4])
                        return e.matmul(ps[:, 8:12], self.onesF[0:4, :], wdg[:, 0:4], start=True, stop=True)
                    P.op("pe", pe3, reads=[r_R0, r_NB, r_wdg, self.r_cst], writes=[r_ps])
                    P.op("dve", lambda e, c=c, ps=ps, CL=CL: e.tensor_copy(CL[:, c * 12:(c + 1) * 12], ps[:, 0:12]), reads=[r_ps], writes=[r_CL])
            for _d in range(2):
                prep_body(_d)

            def head_body(hh):
                gla = hh >= 4
                h = hh % 4
                if not gla:
                    for which, dstb, r_dst in ((0, QC, r_QC), (1, KCc, r_KC)):
                        idx = which * 4 + h
                        for p0 in range(0, S, 1024):
                            n = min(1024, S - p0)
                            lo = 1 if p0 == 0 else 0
                            hi = 1 if p0 + n == S else 0
                            if lo:
                                P.op("dve", lambda e: e.memset(xin[:, 0:1], 0.0), writes=[r_xin])
                            if hi:
                                P.op("dve", lambda e, n=n: e.memset(xin[:, n + 1:n + 2], 0.0), writes=[r_xin])
                            P.dma("sp", lambda e, idx=idx, p0=p0, n=n, lo=lo, hi=hi: e.dma_start(
                                out=xin[:, lo:n + 2 - hi], in_=self.PF[idx, :, seg0 + p0 - 1 + lo:seg0 + p0 + n + 1 - hi]), writes=[r_xin], semkey="xin")
                            P.op("dve", lambda e, n=n, idx=idx: e.tensor_scalar(out=acc[:, 0:n], in0=xin[:, 1:n + 1], scalar1=self.convw(1, idx), scalar2=self.convw(3, idx),
                                                                              op0=ALU.mult, op1=ALU.add), reads=[r_xin, self.r_sm], writes=[r_acc])
                            P.op("dve", lambda e, n=n, idx=idx: e.scalar_tensor_tensor(out=acc[:, 0:n], in0=xin[:, 0:n], scalar=self.convw(0, idx), in1=acc[:, 0:n],
                                                                                      op0=ALU.mult, op1=ALU.add), reads=[r_xin, r_acc, self.r_sm], writes=[r_acc])
                            P.op("dve", lambda e, n=n, idx=idx: e.scalar_tensor_tensor(out=acc[:, 0:n], in0=xin[:, 2:n + 2], scalar=self.convw(2, idx), in1=acc[:, 0:n],
                                                                                      op0=ALU.mult, op1=ALU.add), reads=[r_xin, r_acc, self.r_sm], writes=[r_acc])
                            if which == 0:
                                P.op("act", lambda e, n=n, p0=p0, dstb=dstb: e.activation(out=dstb[:, p0:p0 + n], in_=acc[:, 0:n], func=AF.Silu),
                                     reads=[r_acc], writes=[r_dst])
                            else:
                                P.op("act", lambda e, n=n: e.activation(out=acc[:, 0:n], in_=acc[:, 0:n], func=AF.Silu), reads=[r_acc], writes=[r_acc])
                                P.op("dve", lambda e, n=n, p0=p0, dstb=dstb: e.tensor_scalar(out=dstb[:, p0:p0 + n], in0=acc[:, 0:n], scalar1=128.0 ** -0.5, scalar2=None,
                                                                                         op0=ALU.mult), reads=[r_acc], writes=[r_dst])
                    for c in range(NC):
                        ps, r_ps = pss.next()
                        psb = ps[:].bitcast(BF16)
                        P.op("pe", lambda e, c=c, psb=psb: e.transpose(psb[:, 0:128], KCc[:, c * 128:(c + 1) * 128], self.identB[:]),
                             reads=[r_KC, self.r_identB], writes=[r_ps])
                        P.op("act", lambda e, c=c, psb=psb: e.activation(out=Ktok[:, c * 128:(c + 1) * 128], in_=psb[:, 0:128], func=AF.Copy),
                             reads=[r_ps], writes=[r_Ktok])
                    Vx3 = Vx[:, 0:NC * 257].rearrange("p (c e) -> p c e", e=257)
                    P.op("pool", lambda e, Vx3=Vx3: e.memset(Vx3[:, :, 256:257], 1.0), writes=[r_Vx])
                    for c8 in range(0, NC, 8):
                        ce = min(NC, c8 + 8)
                        P.dma("pool", lambda e, Vx3=Vx3, h=h, c8=c8, ce=ce: e.dma_start(
                            out=Vx3[:, c8:ce, 0:256], in_=self.PT[seg0 + c8 * 128:seg0 + ce * 128, h * 256:(h + 1) * 256].rearrange("(c p) e -> p c e", p=128)),
                            writes=[r_Vx], semkey="vx")
                else:
                    Vg3 = Vx[:, 0:NC * 256].rearrange("p (c e) -> p c e", e=256)
                    for c8 in range(0, NC, 8):
                        ce = min(NC, c8 + 8)
                        P.dma("pool", lambda e, Vg3=Vg3, h=h, c8=c8, ce=ce: e.dma_start(
                            out=Vg3[:, c8:ce, :], in_=self.PT[seg0 + c8 * 128:seg0 + ce * 128, 2048 + h * 256:2048 + (h + 1) * 256].rearrange("(c p) e -> p c e", p=128)),
                            writes=[r_Vx], semkey="vx")

                def dir_body(d):
                    CL, r_CL = COLS[d]
                    order = list(range(NC)) if d == 0 else list(range(NC - 1, -1, -1))
                    msk = self.maskF[d]
                    if not gla:
                        if is_s:
                            P.dma("sp", lambda e, d=d, h=h: e.dma_start(out=Cst[:, 0:256], in_=self.st_C[d, h]), writes=[r_Cst], semkey="cst")
                            P.dma("sp", lambda e, d=d, h=h: e.dma_start(out=Cst[:, 256:257], in_=self.st_n[d, h].rearrange("(p o) -> p o", o=1)), writes=[r_Cst], semkey="cst")
                        else:
                            P.op("dve", lambda e: e.memset(Cst[:, 0:257], 0.0), writes=[r_Cst])
                    else:
                        if is_s:
                            P.dma("sp", lambda e, d=d, h=h: e.dma_start(out=Sst, in_=self.st_S[d, h]), writes=[r_Sst], semkey="sst")
                        else:
                            P.op("dve", lambda e: e.memset(Sst, 0.0), writes=[r_Sst])
                    def chunk_body(c):
                        tc = slice(c * 128, (c + 1) * 128)
                        g0 = seg0 + c * 128
                        rk = r_HMD.setdefault((hh, g0), Res())
                        if not gla:
                            ps1, r_ps1 = pss.next()
                            P.op("pe", lambda e, ps1=ps1, tc=tc: e.matmul(ps1[:, 0:128], KCc[:, tc], QC[:, tc], start=True, stop=True),
                                 reads=[r_KC, r_QC], writes=[r_ps1])
                            stm, r_stm = b128.next()
                            P.op("dve", lambda e, ps1=ps1, stm=stm, msk=msk: e.tensor_tensor(out=stm, in0=ps1[:, 0:128], in1=msk, op=ALU.mult),
                                 reads=[r_ps1, self.r_cst], writes=[r_stm])
                            vw, r_vw = b260.next()
                            P.op("pool", lambda e, vw=vw, c=c, h=h, CL=CL: e.tensor_scalar(out=vw[:, 0:257], in0=Vx[:, c * 257:(c + 1) * 257], scalar1=CL[:, c * 12 + h:c * 12 + h + 1],
                                                                                         scalar2=None, op0=ALU.mult), reads=[r_Vx, r_CL], writes=[r_vw])
                            P.op("dve", lambda e, c=c, h=h, CL=CL: e.tensor_scalar(out=Cst[:, 0:257], in0=Cst[:, 0:257], scalar1=CL[:, c * 12 + 8 + h:c * 12 + 9 + h], scalar2=None, op0=ALU.mult),
                                 reads=[r_Cst, r_CL], writes=[r_Cst])
                            cb, r_cb = b260.next()
                            P.op("act", lambda e, cb=cb: e.activation(out=cb[:, 0:257], in_=Cst[:, 0:257], func=AF.Copy), reads=[r_Cst], writes=[r_cb])
                            pso, r_pso = pss.next()

                            def mo(e, pso=pso, cb=cb, stm=stm, vw=vw, tc=tc):
                                e.matmul(pso[:, 0:257], QC[:, tc], cb[:, 0:257], start=True, stop=False)
                                return e.matmul(pso[:, 0:257], stm, vw[:, 0:257], start=False, stop=True)
                            P.op("pe", mo, reads=[r_QC, r_cb, r_stm, r_vw], writes=[r_pso])
                            psc, r_psc = pss.next()
                            P.op("pe", lambda e, psc=psc, vw=vw, tc=tc: e.matmul(psc[:, 0:257], Ktok[:, tc], vw[:, 0:257], start=True, stop=True),
                                 reads=[r_Ktok, r_vw], writes=[r_psc])
                            P.op("dve", lambda e, psc=psc: e.tensor_tensor(out=Cst[:, 0:257], in0=psc[:, 0:257], in1=Cst[:, 0:257], op=ALU.add),
                                 reads=[r_psc, r_Cst], writes=[r_Cst])
                            dn, r_dn = small.next()
                            P.op("act", lambda e, pso=pso, dn=dn: e.activation(out=dn[:, 0:1], in_=pso[:, 256:257], func=AF.Abs), reads=[r_pso], writes=[r_dn])
                            P.op("dve", lambda e, dn=dn, c=c, h=h, CL=CL: e.tensor_tensor(out=dn[:, 0:1], in0=dn[:, 0:1], in1=CL[:, c * 12 + 4 + h:c * 12 + 5 + h], op=ALU.max),
                                 reads=[r_dn, r_CL], writes=[r_dn])
                            P.op("dve", lambda e, dn=dn: e.reciprocal(out=dn[:, 0:1], in_=dn[:, 0:1]), reads=[r_dn], writes=[r_dn])
                            ho, r_ho = hrot.next()
                            P.op("dve", lambda e, pso=pso, dn=dn, ho=ho: e.tensor_scalar(out=ho, in0=pso[:, 0:256], scalar1=dn[:, 0:1], scalar2=None, op0=ALU.mult),
                                 reads=[r_pso, r_dn], writes=[r_ho])
                        else:
                            lr, r_lr = lrt.next()
                            P.dma("sp", lambda e, lr=lr, d=d, g0=g0: e.dma_start(out=lr[0:16, :], in_=self.GRb[d][:, g0:g0 + 128]), writes=[r_lr], semkey=self.lrkey(lr))
                            psg, r_psg = pss.next()
                            P.op("pe", lambda e, psg=psg, lr=lr, d=d, h=h: e.matmul(psg[:, 0:128], lr, gk[:, d * 512 + h * 128:d * 512 + (h + 1) * 128], start=True, stop=True),
                                 reads=[r_lr, r_gk], writes=[r_psg])
                            nla, r_nla = f128.next()
                            P.op("act", lambda e, psg=psg, nla=nla: e.activation(out=nla, in_=psg[:, 0:128], func=AF.Exp, scale=-1.0), reads=[r_psg], writes=[r_nla])
                            P.op("act", lambda e, nla=nla: e.activation(out=nla, in_=nla, func=AF.Ln, scale=1.0, bias=self.onesF[:, 0:1]), reads=[r_nla, self.r_cst], writes=[r_nla])
                            psb_, r_psb = pss.next()
                            P.op("pe", lambda e, psb_=psb_, nla=nla, msk=msk: e.matmul(psb_[:, 0:128], nla, msk, start=True, stop=True),
                                 reads=[r_nla, self.r_cst], writes=[r_psb])
                            eq, r_eq = f128.next()
                            ek, r_ek = f128.next()
                            P.op("act", lambda e, psb_=psb_, eq=eq: e.activation(out=eq, in_=psb_[:, 0:128], func=AF.Exp, scale=-1.0 / 16.0), reads=[r_psb], writes=[r_eq])
                            P.op("act", lambda e, psb_=psb_, ek=ek: e.activation(out=ek, in_=psb_[:, 0:128], func=AF.Exp, scale=1.0 / 16.0), reads=[r_psb], writes=[r_ek])
                            qr, r_qr = f128.next()
                            kr, r_kr = f128.next()
                            P.dma("sp", lambda e, qr=qr, h=h, g0=g0: e.dma_start(out=qr, in_=self.PF[8 + h, :, g0:g0 + 128]), writes=[r_qr], semkey=self.lrkey(qr))
                            P.dma("sp", lambda e, kr=kr, h=h, g0=g0: e.dma_start(out=kr, in_=self.PF[12 + h, :, g0:g0 + 128]), writes=[r_kr], semkey=self.lrkey(kr))
                            qt, r_qt = b128.next()
                            kt, r_kt = b128.next()
                            P.op("dve", lambda e, qt=qt, qr=qr, eq=eq: e.scalar_tensor_tensor(out=qt, in0=qr, scalar=128.0 ** -0.5, in1=eq, op0=ALU.mult, op1=ALU.mult),
                                 reads=[r_qr, r_eq], writes=[r_qt])
                            P.op("dve", lambda e, kt=kt, kr=kr, ek=ek: e.tensor_tensor(out=kt, in0=kr, in1=ek, op=ALU.mult), reads=[r_kr, r_ek], writes=[r_kt])
                            pst, r_pst = pss.next()
                            pstb = pst[:].bitcast(BF16)
                            P.op("pe", lambda e, pstb=pstb, kt=kt: e.transpose(pstb[:, 0:128], kt, self.identB[:]), reads=[r_kt, self.r_identB], writes=[r_pst])
                            ktk, r_ktk = b128.next()
                            P.op("act", lambda e, pstb=pstb, ktk=ktk: e.activation(out=ktk, in_=pstb[:, 0:128], func=AF.Copy), reads=[r_pst], writes=[r_ktk])
                            psa, r_psa = pss.next()
                            P.op("pe", lambda e, psa=psa, kt=kt, qt=qt: e.matmul(psa[:, 0:128], kt, qt, start=True, stop=True), reads=[r_kt, r_qt], writes=[r_psa])
                            am, r_am = b128.next()
                            P.op("dve", lambda e, psa=psa, am=am, msk=msk: e.tensor_tensor(out=am, in0=psa[:, 0:128], in1=msk, op=ALU.mult), reads=[r_psa, self.r_cst], writes=[r_am])
                            sb_, r_sb = b256.next()
                            P.op("act", lambda e, sb_=sb_: e.activation(out=sb_, in_=Sst, func=AF.Copy), reads=[r_Sst], writes=[r_sb])
                            pso, r_pso = pss.next()
                            vc = Vx[:, c * 256:(c + 1) * 256]

                            def mo(e, pso=pso, qt=qt, sb_=sb_, am=am, vc=vc):
                                e.matmul(pso[:, 0:256], qt, sb_, start=True, stop=False)
                                return e.matmul(pso[:, 0:256], am, vc, start=False, stop=True)
                            P.op("pe", mo, reads=[r_qt, r_sb, r_am, r_Vx], writes=[r_pso])
                            pss2, r_pss2 = pss.next()
                            P.op("pe", lambda e, pss2=pss2, ktk=ktk, vc=vc: e.matmul(pss2[:, 0:256], ktk, vc, start=True, stop=True), reads=[r_ktk, r_Vx], writes=[r_pss2])
                            P.op("dve", lambda e, pss2=pss2: e.tensor_tensor(out=Sst, in0=pss2[:, 0:256], in1=Sst, op=ALU.add), reads=[r_pss2, r_Sst], writes=[r_Sst])
                            ec = 127 if d == 0 else 0
                            P.op("dve", lambda e, eq=eq, ec=ec: e.tensor_scalar(out=Sst, in0=Sst, scalar1=eq[:, ec:ec + 1], scalar2=None, op0=ALU.mult),
                                 reads=[r_Sst, r_eq], writes=[r_Sst])
                            ho, r_ho = hrot.next()
                            P.op("act", lambda e, pso=pso, ho=ho: e.activation(out=ho, in_=pso[:, 0:256], func=AF.Copy), reads=[r_pso], writes=[r_ho])
                        if d == 0:
                            P.dma("sp", lambda e, ho=ho, g0=g0, hh=hh: e.dma_start(out=self.HMD[g0:g0 + 128, hh, :], in_=ho), reads=[r_ho], writes=[rk], semkey=self.lrkey(ho))
                        else:
                            hf, r_hf = hrot.next()
                            P.dma("sp", lambda e, hf=hf, g0=g0, hh=hh: e.dma_start(out=hf, in_=self.HMD[g0:g0 + 128, hh, :]), reads=[rk], writes=[r_hf], semkey=self.lrkey(hf))
                            P.op("dve", lambda e, ho=ho, hf=hf: e.tensor_tensor(out=ho, in0=ho, in1=hf, op=ALU.add), reads=[r_ho, r_hf], writes=[r_ho])
                            ss, r_ss = small.next()
                            P.op("act", lambda e, hf=hf, ho=ho, ss=ss: e.activation(out=hf, in_=ho, func=AF.Square, accum_out=ss[:, 0:1]), reads=[r_ho], writes=[r_hf, r_ss])
                            P.op("act", lambda e, ss=ss: e.activation(out=ss[:, 0:1], in_=ss[:, 0:1], func=AF.Sqrt, scale=1.0 / 256.0, bias=self.eps_ap()), reads=[r_ss, self.r_eps], writes=[r_ss])
                            P.op("dve", lambda e, ss=ss: e.reciprocal(out=ss[:, 0:1], in_=ss[:, 0:1]), reads=[r_ss], writes=[r_ss])
                            gt_, r_gt = grot.next()
                            gcol = (1024 if not gla else 3072) + h * 256
                            P.dma("sp", lambda e, gt_=gt_, g0=g0, gcol=gcol: e.dma_start(out=gt_, in_=self.PT[g0:g0 + 128, gcol:gcol + 256]), writes=[r_gt], semkey=self.lrkey(gt_))
                            P.op("act", lambda e, gt_=gt_, gla=gla: e.activation(out=gt_, in_=gt_, func=(AF.Silu if gla else AF.Sigmoid)), reads=[r_gt], writes=[r_gt])
                            yb, r_yb = b256.next()
                            P.op("dve", lambda e, ho=ho, ss=ss, gt_=gt_, yb=yb: e.scalar_tensor_tensor(out=yb, in0=ho, scalar=ss[:, 0:1], in1=gt_, op0=ALU.mult, op1=ALU.mult),
                                 reads=[r_ho, r_ss, r_gt], writes=[r_yb])
                            for j in range(2):
                                pt_, r_pt = pss.next()
                                ptb = pt_[:].bitcast(BF16)
                                P.op("pe", lambda e, ptb=ptb, yb=yb, j=j: e.transpose(ptb[:, 0:128], yb[:, j * 128:(j + 1) * 128], self.identB[:]),
                                     reads=[r_yb, self.r_identB], writes=[r_pt])
                                yt, r_yt = b128.next()
                                ci = hh * 2 + j
                                P.op("dve", lambda e, ptb=ptb, yt=yt, ci=ci: e.tensor_scalar(out=yt, in0=ptb[:, 0:128], scalar1=self.sm[:, 32 + ci:33 + ci], scalar2=None, op0=ALU.mult),
                                     reads=[r_pt, self.r_sm], writes=[r_yt])
                                P.dma("sp", lambda e, yt=yt, ci=ci, g0=g0: e.dma_start(out=self.YT[ci, :, g0:g0 + 128], in_=yt), reads=[r_yt], semkey=self.lrkey(yt))
                    for _c in order:
                        chunk_body(_c)
                    if not is_s:
                        q = kind[1]
                        if not gla:
                            P.dma("sp", lambda e, q=q, d=d, h=h: e.dma_start(out=self.o_C[q, d, h], in_=Cst[:, 0:256]), reads=[r_Cst], semkey="ocs")
                            P.dma("sp", lambda e, q=q, d=d, h=h: e.dma_start(out=self.o_n[q, d, h].rearrange("(p o) -> p o", o=1), in_=Cst[:, 256:257]), reads=[r_Cst], semkey="ocs")
                        else:
                            P.dma("sp", lambda e, q=q, d=d, h=h: e.dma_start(out=self.o_S[q, d, h], in_=Sst), reads=[r_Sst], semkey="oss")
                for _d in range(2):
                    dir_body(_d)
            for _hh in getattr(self, 'heads', range(8)):
                head_body(_hh)
        for _sg in segs:
            seg_body(*_sg)
        P.barrier()

    @staticmethod
    def bc_last(v, n):
        a = v.ap
        return bass.AP(v.tensor, v.offset, [list(a[0]), list(a[1]), [0, n]])

    def lrkey(self, v):
        k = id(v)
        if not hasattr(self, "_lrk"):
            self._lrk = {}
        if k not in self._lrk:
            self._lrk[k] = "tk%d" % len(self._lrk)
            self._lrv = getattr(self, "_lrv", [])
            self._lrv.append(v)
        return self._lrk[k]

    def build_test_mixer(self):
        self.declare_io()
        self.declare_io2()
        self.alloc_eps()
        self.alloc_common()
        self.alloc_consts()
        self.modulation(0)
        for blk in range(self.NBLK):
            self.load_x(blk, self.xT_in)
            self.adaln(blk, 0, 1)
            self.inproj(blk)
        if not getattr(self, 'skip_core', False):
            self.mixer_core(self.segs)
        self.dbgY = self.dram_out("dbgY", [16, 128, self.T], BF16)
        stg, r_stg = self.hT[:, 0, :], Res()
        TB = self.TB
        for ci in range(16):
            for bk in range(self.NBLK):
                self.P.dma("sp", lambda e, ci=ci, bk=bk: e.dma_start(out=stg[:, 0:TB], in_=self.YT[ci][:, bk * TB:(bk + 1) * TB]), writes=[r_stg], semkey="dbg1")
                self.P.dma("sp", lambda e, ci=ci, bk=bk: e.dma_start(out=self.dbgY[ci][:, bk * TB:(bk + 1) * TB], in_=stg[:, 0:TB]), reads=[r_stg], semkey="dbg2")
        self.P.emit()
        self.stack.close()
        return self.nc

    def outproj(self, blk, wdram, src, l):
        P = self.P
        TB = self.TB
        tok0 = blk * TB
        g = self.key("g")
        for j in range(KC):
            P.dma("sp", lambda e, j=j: e.dma_start(out=self.hT[:, j, :], in_=src[j, :, tok0:tok0 + TB]), writes=self.rh[j], semkey="hload", group=g)
        wv = wdram.rearrange("(k p) c -> p k c", p=128)
        for npair in range(8):
            wt, r_wt, wkey = self.wgu.next()
            P.dma("pool", lambda e, wt=wt, npair=npair: e.dma_start(out=wt[:], in_=wv[:, :, npair * 256:(npair + 1) * 256]), writes=[r_wt], semkey=wkey)
            for cc in range(2):
                n = npair * 2 + cc
                for t in range(self.NT):
                    cv = self.cv_of(blk, t)
                    ts = slice(t * 512, (t + 1) * 512)
                    ps, r_ps = self.rotB.next()

                    def mm(e, wt=wt, cc=cc, ts=ts, ps=ps):
                        ins = None
                        for j in range(KC):
                            ins = e.matmul(ps[:], wt[:, j, cc * 128:(cc + 1) * 128], self.hT[:, j, ts], start=(j == 0), stop=(j == KC - 1))
                        return ins
                    P.op("pe", mm, reads=[r_wt] + [self.rh[j][t] for j in range(KC)], writes=[r_ps])
                    P.op("dve", lambda e, ps=ps, n=n, ts=ts, cv=cv: e.scalar_tensor_tensor(
                        out=self.xT[:, n, ts], in0=ps[:], scalar=self.gt[:, l, 1, cv, n:n + 1], in1=self.xT[:, n, ts], op0=ALU.mult, op1=ALU.add),
                        reads=[r_ps, self.r_s1[l], self.rx[n][t]], writes=[self.rx[n][t]])

    def qkvproj(self, blk):
        P = self.P
        TB = self.TB
        tok0 = blk * TB
        P.barrier()
        wv = self.w_qkv[0].rearrange("(k p) c -> p k c", p=128)
        f0 = self.wd.items[0][0][:].rearrange("p a b -> p (a b)").bitcast(F32)
        f1 = self.wd.items[1][0][:].rearrange("p a b -> p (a b)").bitcast(F32)
        tl = [(f0[:, i * 512:(i + 1) * 512], Res()) for i in range(4)] + [(f1[:, i * 512:(i + 1) * 512], Res()) for i in range(4)]
        qn_r = Rot(tl[0:2])
        cs_r = Rot(tl[2:4])
        sn_r = Rot(tl[4:6])
        t1, r_t1 = tl[6]
        t2, r_t2 = tl[7]
        ab = self.aT[:].rearrange("p a b -> p (a b)")
        ost = Rot([(ab[:, i * 512:(i + 1) * 512], Res(), "ost%d" % i) for i in range(2 * TB // 512)])
        gcol = [self.sm[:, 48:49], self.sm[:, 49:50]]

        def dst(ci, t, ps, r_ps):
            isq = ci < 16
            gtok = tok0 + t * 512
            sq, r_sq = self.sq.next()
            P.op("act", lambda e: e.activation(out=sq[:], in_=ps[:], func=AF.Square), reads=[r_ps], writes=[r_sq])
            pn, r_pn = self.rotB.next()
            P.op("pe", lambda e: e.matmul(pn[:], self.ones_bf[:], sq[:], start=True, stop=True), reads=[r_sq, self.r_ones], writes=[r_pn])
            P.op("act", lambda e: e.activation(out=self.rstd[:], in_=pn[:], func=AF.Sqrt, scale=1.0 / 128.0, bias=self.eps_ap()),
                 reads=[r_pn, self.r_eps], writes=[self.r_rstd])
            P.op("dve", lambda e: e.reciprocal(out=self.rstd[:], in_=self.rstd[:]), reads=[self.r_rstd], writes=[self.r_rstd])
            o, r_o, ko = ost.next()
            dstap = (self.QT[ci] if isq else self.KT[ci - 16])[:, gtok:gtok + 512]
            if gtok < 512 or getattr(self, 'skip_rope', False):
                P.op("dve", lambda e: e.scalar_tensor_tensor(out=o, in0=ps[:], scalar=gcol[0 if isq else 1], in1=self.rstd[:], op0=ALU.mult, op1=ALU.mult),
                     reads=[r_ps, self.r_sm, self.r_rstd], writes=[r_o])
            else:
                pos0 = gtok - 512
                qn, r_qn = qn_r.next()
                P.op("dve", lambda e: e.scalar_tensor_tensor(out=qn, in0=ps[:], scalar=gcol[0 if isq else 1], in1=self.rstd[:], op0=ALU.mult, op1=ALU.mult),
                     reads=[r_ps, self.r_sm, self.r_rstd], writes=[r_qn])
                cs, r_cs = cs_r.next()
                sn, r_sn = sn_r.next()
                P.dma("sp", lambda e: e.dma_start(out=cs, in_=self.ropeT[0, :, pos0:pos0 + 512]), writes=[r_cs], semkey=self.lrkey(cs))
                P.dma("sp", lambda e: e.dma_start(out=sn, in_=self.ropeT[1, :, pos0:pos0 + 512]), writes=[r_sn], semkey=self.lrkey(sn))
                pr, r_pr = self.rotB.next()
                P.op("pe", lambda e: e.matmul(pr[:], self.ropeR, qn, start=True, stop=True), reads=[r_qn, self.r_cst], writes=[r_pr])
                P.op("dve", lambda e: e.tensor_tensor(out=t1, in0=qn, in1=cs, op=ALU.mult), reads=[r_qn, r_cs], writes=[r_t1])
                P.op("dve", lambda e: e.tensor_tensor(out=t2, in0=pr[:], in1=sn, op=ALU.mult), reads=[r_pr, r_sn], writes=[r_t2])
                P.op("dve", lambda e: e.tensor_tensor(out=o, in0=t1, in1=t2, op=ALU.add), reads=[r_t1, r_t2], writes=[r_o])
            P.dma("sp", lambda e: e.dma_start(out=dstap, in_=o), reads=[r_o], semkey=ko)
        if not getattr(self, 'skip_qk', False):
            self.proj_fm(wv, 0, 2560, dst)

        def dstv(tg, ps, r_ps):
            gtok = tok0 + tg * 128
            o, r_o, ko = ost.next()
            if gtok < 512:
                qn, r_qn = qn_r.next()
                P.op("dve", lambda e: e.tensor_copy(qn, ps[:]), reads=[r_ps], writes=[r_qn])
                P.op("act", lambda e: e.activation(out=o, in_=qn, func=AF.Copy), reads=[r_qn], writes=[r_o])
                P.dma("sp", lambda e: e.dma_start(out=self.o_v[gtok:gtok + 128, :], in_=qn), reads=[r_qn], semkey=self.lrkey(qn))
            else:
                P.op("act", lambda e: e.activation(out=o, in_=ps[:], func=AF.Copy), reads=[r_ps], writes=[r_o])
            P.dma("sp", lambda e: e.dma_start(out=self.VT[gtok:gtok + 128, :], in_=o), reads=[r_o], semkey=ko)
        if not getattr(self, 'skip_v', False):
            self.proj_tm(wv, 2560, dstv)
        if blk == 0 and not getattr(self, 'skip_k', False):
            kgb, r_kgb = tl[6]
            kr = self.qkg_row[1:2, :]
            src_b = bass.AP(kr.tensor, kr.offset, [[0, 128], [1, 128]])
            P.dma("sp", lambda e: e.dma_start(out=kgb[:, 0:128], in_=src_b), writes=[r_kgb], semkey=self.lrkey(kgb))
            ssk, r_ssk = tl[7][0][:, 0:8], tl[7][1]

            def dstk(tg, ps, r_ps):
                gtok = tok0 + tg * 128
                junk, r_junk = cs_r.next()
                for h in range(4):
                    P.op("act", lambda e, h=h: e.activation(out=junk[:, 0:128], in_=ps[:, h * 128:(h + 1) * 128], func=AF.Square, accum_out=ssk[:, h:h + 1]),
                         reads=[r_ps], writes=[r_junk, r_ssk])
                P.op("act", lambda e: e.activation(out=ssk[:, 0:4], in_=ssk[:, 0:4], func=AF.Sqrt, scale=1.0 / 128.0, bias=self.eps_ap()),
                     reads=[r_ssk, self.r_eps], writes=[r_ssk])
                P.op("dve", lambda e: e.reciprocal(out=ssk[:, 0:4], in_=ssk[:, 0:4]), reads=[r_ssk], writes=[r_ssk])
                qn, r_qn = qn_r.next()
                for h in range(4):
                    P.op("dve", lambda e, h=h: e.scalar_tensor_tensor(out=qn[:, h * 128:(h + 1) * 128], in0=ps[:, h * 128:(h + 1) * 128], scalar=ssk[:, h:h + 1],
                                                                      in1=kgb[:, 0:128], op0=ALU.mult, op1=ALU.mult), reads=[r_ps, r_ssk, r_kgb], writes=[r_qn])
                P.dma("sp", lambda e: e.dma_start(out=self.o_k[gtok:gtok + 128, :], in_=qn), reads=[r_qn], semkey=self.lrkey(qn))
            self.proj_tm(wv, 2048, dstk, tgs=range(4))
        P.barrier()

    def attn_core(self, segs):
        P = self.P
        P.barrier()
        self.arena_reset()
        NKM = max(sg[1] + (256 if sg[2][0] == 's' else 0) for sg in segs)
        KTa, r_KTa = self.t16(NKM)
        Va, r_Va = self.t16(NKM)
        qrot = Rot([self.t16(512) for _ in range(2)])
        prot = Rot([self.t16(512) for _ in range(4)])
        orot = Rot([self.t16(512) for _ in range(2)])
        rsrot = Rot([self.t32(512) for _ in range(2)])
        negb, r_negb = self.t32(4)
        gr, r_gr = self.t32(256, parts=1)
        gm, r_gm = self.t32(4, parts=1)
        P.dma("sp", lambda e: e.dma_start(out=gr, in_=self.qkg_row.rearrange("(o a) b -> o (a b)", o=1)), writes=[r_gr], semkey=self.key("c"))
        P.op("dve", lambda e: e.tensor_reduce(out=gm[:, 0:2], in_=gr.rearrange("p (a b) -> p a b", a=2), axis=AX.X, op=ALU.max, apply_absolute_value=True),
             reads=[r_gr], writes=[r_gm])
        P.op("dve", lambda e: e.tensor_tensor(out=gm[:, 2:3], in0=gm[:, 0:1], in1=gm[:, 1:2], op=ALU.mult), reads=[r_gm], writes=[r_gm])
        P.op("dve", lambda e: e.tensor_scalar(out=gm[:, 2:3], in0=gm[:, 2:3], scalar1=-(128.0 ** 0.5), scalar2=None, op0=ALU.mult), reads=[r_gm], writes=[r_gm])
        pb, r_pb = self.psM
        P.op("pe", lambda e: e.matmul(pb[:, 0:1], self.onesF[0:1, :], gm[:, 2:3], start=True, stop=True), reads=[r_gm, self.r_cst], writes=[r_pb])
        P.op("dve", lambda e: e.tensor_copy(negb[:, 0:1], pb[:, 0:1]), reads=[r_pb], writes=[r_negb])
        psS = Rot(self.psA)
        psP = Rot([(self.psB[0], self.psB[1]), (self.psB[2], self.psM)])
        scale = 128.0 ** -0.5

        def seg_body(tok0, S, kind):
            is_s = kind[0] == 's'
            off = 256 if is_s else 0
            NK = S + off
            NKT = NK // 128
            QB = 512 if S >= 512 else S

            def kv_body(kvh):
                if is_s:
                    P.dma("pool", lambda e: e.dma_start(out=KTa[:, 0:256], in_=self.cacheKT[kvh]), writes=[r_KTa], semkey="kta")
                    P.dma("pool", lambda e: e.dma_start(out=Va[:, 0:256].rearrange("p (c d) -> p c d", d=128),
                                                        in_=self.cacheV[:, kvh, :].rearrange("(c p) d -> p c d", p=128)), writes=[r_Va], semkey="va")
                for s0 in range(0, S, 1024):
                    s1 = min(S, s0 + 1024)
                    P.dma("sp", lambda e, s0=s0, s1=s1: e.dma_start(out=KTa[:, off + s0:off + s1], in_=self.KT[kvh][:, tok0 + s0:tok0 + s1]), writes=[r_KTa], semkey="kta2")
                    P.dma("sp", lambda e, s0=s0, s1=s1: e.dma_start(out=Va[:, off + s0:off + s1].rearrange("p (c d) -> p c d", d=128),
                                                                  in_=self.VT[tok0 + s0:tok0 + s1, kvh * 128:(kvh + 1) * 128].rearrange("(c p) d -> p c d", p=128)),
                          writes=[r_Va], semkey="va2")

                def q_body(head, qb):
                    q0 = tok0 + qb * QB
                    qT, r_qT = qrot.next()
                    P.dma("sp", lambda e: e.dma_start(out=qT[:, 0:QB], in_=self.QT[head][:, q0:q0 + QB]), writes=[r_qT], semkey=self.lrkey(qT))
                    (po, r_po), (pz, r_pz) = psP.next()
                    pend = []

                    def pv(kt, pt):
                        ptile, r_pt = pt

                        def f(e):
                            e.matmul(po[:, 0:QB], Va[:, kt * 128:(kt + 1) * 128], ptile[:, 0:QB], start=(kt == 0), stop=(kt == NKT - 1))
                            return e.matmul(pz[:, 0:QB], self.ones_bf[:], ptile[:, 0:QB], start=(kt == 0), stop=(kt == NKT - 1))
                        P.op("pe", f, reads=[r_Va, r_pt, self.r_ones], writes=[r_po, r_pz])

                    def sc(kt):
                        ps, r_ps = psS.next()
                        P.op("pe", lambda e: e.matmul(ps[:, 0:QB], KTa[:, kt * 128:(kt + 1) * 128], qT[:, 0:QB], start=True, stop=True),
                             reads=[r_KTa, r_qT], writes=[r_ps])
                        pt = prot.next()
                        P.op("act", lambda e: e.activation(out=pt[0][:, 0:QB], in_=ps[:, 0:QB], func=AF.Exp, scale=scale, bias=negb[:, 0:1]),
                             reads=[r_ps, r_negb], writes=[pt[1]])
                        return pt
                    for kt in range(NKT):
                        pend.append((kt, sc(kt)))
                        if len(pend) > 2:
                            pv(*pend.pop(0))
                    while pend:
                        pv(*pend.pop(0))
                    rs, r_rs = rsrot.next()
                    P.op("dve", lambda e: e.reciprocal(out=rs[:, 0:QB], in_=pz[:, 0:QB]), reads=[r_pz], writes=[r_rs])
                    o, r_o = orot.next()
                    P.op("dve", lambda e: e.tensor_tensor(out=o[:, 0:QB], in0=po[:, 0:QB], in1=rs[:, 0:QB], op=ALU.mult), reads=[r_po, r_rs], writes=[r_o])
                    P.dma("sp", lambda e: e.dma_start(out=self.OT[head][:, q0:q0 + QB], in_=o[:, 0:QB]), reads=[r_o], semkey=self.lrkey(o))
                for g in range(4):
                    for qb in range(S // QB):
                        q_body(kvh * 4 + g, qb)
            for kvh in range(4):
                kv_body(kvh)
        for sg in segs:
            seg_body(*sg)
        P.barrier()

    def build(self):
        self.declare_io()
        self.declare_io2()
        self.alloc_eps()
        self.alloc_common()
        self.alloc_consts()
        NB = self.NBLK
        self.modulation(0)
        for blk in range(NB):
            self.load_x(blk, self.xT_in)
            self.adaln(blk, 0, 0)
            self.ffn(blk, 0, 0)
            self.adaln(blk, 0, 1)
            self.inproj(blk)
            self.store_x(blk, self.XS)
        self.mixer_core(self.segs)
        self.modulation(1)
        for blk in range(NB):
            self.load_x(blk, self.XS)
            self.outproj(blk, self.w_out[0], self.YT, 0)
            self.adaln(blk, 0, 2)
            self.ffn(blk, 0, 1)
            self.adaln(blk, 1, 0)
            self.ffn(blk, 1, 0)
            self.adaln(blk, 1, 1)
            self.qkvproj(blk)
            self.store_x(blk, self.XS)
        self.attn_core(self.segs)
        for blk in range(NB):
            self.load_x(blk, self.XS)
            self.outproj(blk, self.w_o[0], self.OT, 1)
            self.adaln(blk, 1, 2)
            self.ffn(blk, 1, 1)
            self.store_x(blk, self.xT_out)
        self.P.emit()
        self.stack.close()
        return self.nc

    def build_test_attn(self):
        self.declare_io()
        self.declare_io2()
        self.alloc_eps()
        self.alloc_common()
        self.alloc_consts()
        self.modulation(1)
        for blk in range(self.NBLK):
            self.load_x(blk, self.xT_in)
            self.adaln(blk, 1, 1)
            self.qkvproj(blk)
        if not getattr(self, 'skip_core', False):
            self.attn_core(self.segs)
        self.dbgY = self.dram_out("dbgY", [16, 128, self.T], BF16)
        stg, r_stg = self.hT[:].rearrange("p a b -> p (a b)"), Res()
        for ci in range(16):
            self.P.dma("sp", lambda e, ci=ci: e.dma_start(out=stg[:, 0:self.T], in_=(self.QT if getattr(self, 'skip_core', False) else self.OT)[ci]), writes=[r_stg], semkey="dbg1")
            self.P.dma("sp", lambda e, ci=ci: e.dma_start(out=self.dbgY[ci], in_=stg[:, 0:self.T]), reads=[r_stg], semkey="dbg2")
        self.P.emit()
        self.stack.close()
        return self.nc


def host_common(inp):
    f32 = np.float32
    g = lambda k: np.asarray(inp[k], f32)
    cm = {}
    cm["w_mod"] = g("w_mod")
    cm["b_modT"] = np.ascontiguousarray(g("b_mod").reshape(2, NMOD * KC, 128).transpose(0, 2, 1))
    cm["norm_gT"] = np.ascontiguousarray(g("norm_g").reshape(2 * 3 * KC, 128).T)
    cm["ffn_w_gate"] = g("ffn_w_gate")
    cm["ffn_w_up"] = g("ffn_w_up")
    cm["ffn_w_down"] = g("ffn_w_down")
    cm["w_in_ab"] = g("w_in_ab")
    cm["w_out_ab"] = g("w_out_ab")
    cm["w_qkv"] = g("w_qkv")
    cm["w_o"] = g("w_o")
    consts = np.zeros((128, 768), f32)
    consts[:, 0:128] = np.eye(128, dtype=f32)
    ii = np.arange(128)
    consts[:, 128:256] = (ii[:, None] <= ii[None, :]).astype(f32)
    consts[:, 256:384] = (ii[:, None] >= ii[None, :]).astype(f32)
    consts[:, 384:512] = 1.0
    RT = np.zeros((128, 128), f32)
    for d in range(128):
        j = (d % 64) // 32
        k = d + 32 if j == 0 else d - 32
        RT[k, d] = 1.0
    consts[:, 512:640] = RT
    cm["consts"] = consts
    cw = g("mlstm_conv_w")[0]
    cb = g("mlstm_conv_b")[0]
    convT = np.zeros((128, 4, 8), f32)
    for tap in range(3):
        convT[:, tap, :] = cw[tap].reshape(8, 128).T
    convT[:, 3, :] = cb.reshape(8, 128).T
    cm["convT"] = convT
    bi = g("mlstm_b_i")[0]
    bf = g("mlstm_b_f")[0]
    cm["mbias"] = np.ascontiguousarray(np.stack([bi[0], bf[0], bi[1], bf[1]], axis=1))
    og = np.concatenate([g("mlstm_out_g")[0], g("gla_out_g")[0]])
    cm["outgT"] = np.ascontiguousarray(og.reshape(16, 128).T)
    gkwb = np.zeros((17, 2, 512), f32)
    gkwb[0:16] = g("gla_w_gk")[0].transpose(1, 0, 2)
    gkwb[16] = g("gla_b_gk")[0]
    cm["gkwb"] = gkwb
    cm["qkg"] = np.ascontiguousarray(np.stack([g("q_norm_g")[0], g("k_norm_g")[0]], axis=1))
    cm["qkg_row"] = np.ascontiguousarray(np.stack([g("q_norm_g")[0], g("k_norm_g")[0]], axis=0))
    t = np.arange(4096)
    row = (t // 64).astype(f32)
    col = (t % 64).astype(f32)
    inv = (10000.0 ** (-np.arange(0, 64, 2, dtype=f32) / 64.0)).astype(f32)
    cosT = np.zeros((128, 4096), f32)
    sinT = np.zeros((128, 4096), f32)
    for d in range(128):
        a = d // 64
        j = (d % 64) // 32
        i = d % 32
        ang = (row if a == 0 else col) * inv[i]
        cosT[d] = np.cos(ang)
        sinT[d] = np.sin(ang) * (-1.0 if j == 0 else 1.0)
    cm["ropeT"] = np.stack([cosT, sinT], 0)
    return cm


def host_core(inp, cm, c, bsel):
    f32 = np.float32
    g = lambda k: np.asarray(inp[k], f32)
    d = dict(cm)
    d["st_C"] = np.ascontiguousarray(g("state_mlstm_C")[bsel, 0])
    d["st_n"] = np.ascontiguousarray(g("state_mlstm_n")[bsel, 0])
    d["st_m"] = np.ascontiguousarray(g("state_mlstm_m")[bsel, 0])
    d["st_S"] = np.ascontiguousarray(g("state_gla_S")[bsel, 0])
    d["cacheKT"] = np.ascontiguousarray(g("cache_k")[bsel, 0].transpose(1, 2, 0))
    d["cacheV"] = np.ascontiguousarray(g("cache_v")[bsel, 0])
    cT = np.stack([g("c_ctx"), g("c")[bsel]], -1).reshape(KC, 128, 2).transpose(1, 0, 2)
    d["cT"] = np.ascontiguousarray(cT)
    return d


_IN_NAMES = ["xT_in", "cT", "w_mod", "b_modT", "norm_gT", "ffn_w_gate", "ffn_w_up", "ffn_w_down", "w_in_ab", "w_out_ab", "w_qkv", "w_o",
             "consts", "convT", "mbias", "outgT", "gkwb", "qkg", "qkg_row", "st_C", "st_n", "st_m", "st_S", "cacheKT", "cacheV", "ropeT"]
_CACHE = {}


def _get_nc():
    if "nc" not in _CACHE:
        b = Builder(1536, 3)
        b.segs = [(0, 256, ('p', 0)), (256, 256, ('p', 1)), (512, 4096, ('s',))]
        _CACHE["nc"] = b.build()
    return _CACHE["nc"]


def kernel(**inp):
    nc = _get_nc()
    f32 = np.float32
    cm = host_common(inp)
    xp = np.asarray(inp["x_prompt"], f32)
    xs = np.asarray(inp["x_sample"], f32)
    in_maps = []
    for c in range(8):
        bsel = c % 2
        d = host_core(inp, cm, c, bsel)
        x = np.concatenate([xp[2 * c:2 * c + 2].reshape(512, D), xs[bsel]], 0)
        d["xT_in"] = np.ascontiguousarray(x.T)
        in_maps.append({k: d[k] for k in _IN_NAMES})
    res = run_bass_kernel_spmd(nc, in_maps, core_ids=list(range(8)))
    R = res.results
    y_prompt = np.zeros((16, 256, D), f32)
    y_sample = np.zeros((2, 4096, D), f32)
    oC = np.zeros((16, 1, 2, 4, 128, 256), f32)
    on = np.zeros((16, 1, 2, 4, 128), f32)
    om = np.zeros((16, 1, 2, 4), f32)
    oS = np.zeros((16, 1, 2, 4, 128, 256), f32)
    ok = np.zeros((16, 1, 256, 4, 128), f32)
    ov = np.zeros((16, 1, 256, 4, 128), f32)
    for c in range(8):
        y = np.asarray(R[c]["xT_out"]).T
        y_prompt[2 * c:2 * c + 2] = y[:512].reshape(2, 256, D)
        if c < 2:
            y_sample[c] = y[512:]
        oC[2 * c:2 * c + 2, 0] = np.asarray(R[c]["o_C"])
        on[2 * c:2 * c + 2, 0] = np.asarray(R[c]["o_n"])
        om[2 * c:2 * c + 2, 0] = np.asarray(R[c]["o_m"])
        oS[2 * c:2 * c + 2, 0] = np.asarray(R[c]["o_S"])
        ok[2 * c:2 * c + 2, 0] = np.asarray(R[c]["o_k"]).reshape(2, 256, 4, 128)
        ov[2 * c:2 * c + 2, 0] = np.asarray(R[c]["o_v"]).reshape(2, 256, 4, 128)
    return (y_prompt, y_sample, oC, on, om, oS, ok, ov)
```

```python
import contextlib
import numpy as np
import concourse.bass as bass
import concourse.mybir as mybir
from concourse.bass_utils import run_bass_kernel_spmd

F32 = mybir.dt.float32
BF16 = mybir.dt.bfloat16
AF = mybir.ActivationFunctionType
ALU = mybir.AluOpType
AX = mybir.AxisListType

ENGS = ("pe", "dve", "act", "pool", "sp")

D = 2048
KC = 16
DFF = 5632
MC = 44
NMOD = 9
EPS = 1e-6


class Res:
    __slots__ = ("name", "last_w", "readers")

    def __init__(self, name=""):
        self.name = name
        self.last_w = None
        self.readers = []


class Op:
    __slots__ = ("eng", "fn", "deps", "is_dma", "semkey", "ev", "needs_sig", "idx", "name", "group")


class Prog:
    def __init__(self, nc):
        self.nc = nc
        self.ops = {e: [] for e in ENGS}
        self.nops = 0
        self.last_real = {}
        self.last_dma = {}

    def _mk(self, eng, fn, reads, writes, is_dma, semkey, name, group=None):
        op = Op()
        op.group = group
        op.eng = eng
        op.fn = fn
        op.is_dma = is_dma
        op.semkey = semkey
        op.ev = None
        op.needs_sig = False
        op.idx = self.nops
        op.name = name
        self.nops += 1
        deps = []
        for r in reads:
            if r.last_w is not None:
                deps.append(r.last_w)
        for w in writes:
            if w.last_w is not None:
                deps.append(w.last_w)
            deps.extend(w.readers)
        out = []
        seen = set()
        for d in deps:
            if d is op or id(d) in seen:
                continue
            seen.add(id(d))
            if d.eng == eng and eng == "pe" and not d.is_dma and not is_dma:
                continue
            out.append(d)
        op.deps = out
        for r in reads:
            r.readers.append(op)
        for w in writes:
            w.last_w = op
            w.readers = []
        self.ops[eng].append(op)
        self.last_real[eng] = op
        if is_dma:
            self.last_dma[semkey] = op
        return op

    def barrier(self):
        deps = list(self.last_real.values()) + list(self.last_dma.values())
        uniq = []
        seen = set()
        for d in deps:
            if id(d) not in seen:
                seen.add(id(d))
                uniq.append(d)
        for e in ENGS:
            op = Op()
            op.group = None
            op.eng = e
            op.fn = None
            op.is_dma = False
            op.semkey = None
            op.ev = None
            op.needs_sig = False
            op.idx = self.nops
            op.name = "barrier"
            op.deps = list(uniq)
            self.nops += 1
            self.ops[e].append(op)

    capture = None

    def op(self, eng, fn, reads=(), writes=(), name=""):
        if self.capture is not None:
            self.capture.append((eng, fn, list(reads), list(writes), False, None, name, None))
            return None
        return self._mk(eng, fn, reads, writes, False, None, name)

    def dma(self, eng, fn, reads=(), writes=(), semkey=None, name="", group=None):
        assert semkey is not None
        if self.capture is not None:
            self.capture.append((eng, fn, list(reads), list(writes), True, semkey, name, group))
            return None
        return self._mk(eng, fn, reads, writes, True, semkey, name, group)

    def replay_interleaved(self, lists):
        pos = [0] * len(lists)
        total = sum(len(l) for l in lists)
        done = 0
        while done < total:
            best = None
            for i, l in enumerate(lists):
                if pos[i] < len(l):
                    frac = pos[i] / float(len(l))
                    if best is None or frac < best[0]:
                        best = (frac, i)
            i = best[1]
            self._mk(*lists[i][pos[i]])
            pos[i] += 1
            done += 1

    def emit(self, final_wait_eng="sp"):
        nc = self.nc
        all_dma = [o for e in ENGS for o in self.ops[e] if o.is_dma]
        term = Op()
        term.eng = final_wait_eng
        term.fn = None
        term.is_dma = False
        term.semkey = None
        term.ev = None
        term.needs_sig = False
        term.idx = self.nops
        term.name = "term"
        term.group = None
        lastk = {}
        for o in all_dma:
            lastk[o.semkey] = o
        term.deps = list(lastk.values())
        self.ops[final_wait_eng].append(term)
        for e in ENGS:
            for o in self.ops[e]:
                for d in o.deps:
                    d.needs_sig = True
        stack = contextlib.ExitStack()
        esem = {}
        for e in ENGS:
            esem[e] = stack.enter_context(nc.semaphore("s_" + e))
        ksem = {}
        kcnt = {}
        for e in ENGS:
            c = 0
            for o in self.ops[e]:
                if o.is_dma:
                    if o.semkey not in ksem:
                        ksem[o.semkey] = stack.enter_context(nc.semaphore("d_%d" % len(ksem)))
                        kcnt[o.semkey] = 0
                    kcnt[o.semkey] += 16
                    o.ev = (ksem[o.semkey], kcnt[o.semkey])
                elif o.needs_sig:
                    c += 1
                    o.ev = (esem[e], c)
        gmax = {}
        for e in ENGS:
            for o in self.ops[e]:
                if o.is_dma and o.group is not None:
                    kk = (o.semkey, o.group)
                    gmax[kk] = max(gmax.get(kk, 0), o.ev[1])
        for e in ENGS:
            for o in self.ops[e]:
                if o.is_dma and o.group is not None:
                    o.ev = (o.ev[0], gmax[(o.semkey, o.group)])
        self.n_sems = len(ksem) + len(ENGS)
        block = stack.enter_context(nc.Block())

        def run(e, eng):
            waited = {}
            for o in self.ops[e]:
                for d in o.deps:
                    sem, val = d.ev
                    k = id(sem)
                    if waited.get(k, 0) >= val:
                        continue
                    waited[k] = val
                    eng.wait_ge(sem, val)
                if o.fn is None:
                    continue
                ins = o.fn(eng)
                if o.is_dma:
                    ins.then_inc(o.ev[0], 16)
                elif o.needs_sig:
                    ins.then_inc(o.ev[0], 1)

        @block.tensor
        def _(eng):
            run("pe", eng)

        @block.vector
        def _(eng):
            run("dve", eng)

        @block.scalar
        def _(eng):
            run("act", eng)

        @block.gpsimd
        def _(eng):
            run("pool", eng)

        @block.sync
        def _(eng):
            run("sp", eng)

        stack.close()


class Rot:
    def __init__(self, items):
        self.items = items
        self.i = 0

    def next(self):
        it = self.items[self.i % len(self.items)]
        self.i += 1
        return it


class Builder:
    def __init__(self, TB, NBLK, stages="all", dbg=False):
        self.TB = TB
        self.NBLK = NBLK
        self.T = TB * NBLK
        self.NT = TB // 512
        self.stages = stages
        self.dbg = dbg
        self.nc = bass.Bass("TRN2", target_bir_lowering=False)
        self.P = Prog(self.nc)
        self.stack = contextlib.ExitStack()
        self.sem_i = 0

    def sb(self, name, shape, dt):
        return self.stack.enter_context(self.nc.sbuf_tensor(name, shape, dt))

    def ps(self, name, shape, dt=F32):
        return self.stack.enter_context(self.nc.psum_tensor(name, shape, dt))

    def dram_in(self, name, shape, dt=F32):
        return self.nc.dram_tensor(name, list(shape), dt, kind="ExternalInput").ap()

    def dram_out(self, name, shape, dt=F32):
        return self.nc.dram_tensor(name, list(shape), dt, kind="ExternalOutput").ap()

    def dram_tmp(self, name, shape, dt=F32):
        kind = "ExternalOutput" if (self.dbg and dt == F32) else "Internal"
        return self.nc.dram_tensor(name, list(shape), dt, kind=kind).ap()

    def key(self, base):
        self.sem_i += 1
        return "%s_%d" % (base, self.sem_i)

    def declare_io(self):
        T = self.T
        self.xT_in = self.dram_in("xT_in", [D, T])
        self.cT = self.dram_in("cT", [128, KC, 2])
        self.w_mod = self.dram_in("w_mod", [2, D, NMOD * D])
        self.b_modT = self.dram_in("b_modT", [2, 128, NMOD * KC])
        self.norm_gT = self.dram_in("norm_gT", [128, 2 * 3 * KC])
        self.ffn_wg = self.dram_in("ffn_w_gate", [2, 2, D, DFF])
        self.ffn_wu = self.dram_in("ffn_w_up", [2, 2, D, DFF])
        self.ffn_wd = self.dram_in("ffn_w_down", [2, 2, DFF, D])
        self.xT_out = self.dram_out("xT_out", [D, T])

    def alloc_common(self):
        TB = self.TB
        self.xT = self.sb("xT", [128, KC, TB], F32)
        self.rx = [[Res("x%d_%d" % (k, t)) for t in range(self.NT)] for k in range(KC)]
        self.hT = self.sb("hT", [128, KC, TB], BF16)
        self.rh = [[Res("h%d_%d" % (k, t)) for t in range(self.NT)] for k in range(KC)]
        self.ones_bf = self.sb("ones_bf", [128, 128], BF16)
        self.r_ones = Res("ones")
        self.P.op("pool", lambda e: e.memset(self.ones_bf[:], 1.0), writes=[self.r_ones])
        self.modT = self.sb("modT", [128, 2, 2, NMOD * KC], F32)
        self.r_mod = [Res("mod%d" % l) for l in range(2)]
        self.s1 = self.sb("s1", [128, 2, 3, 2, KC], F32)
        self.gt = self.sb("gt", [128, 2, 3, 2, KC], F32)
        self.r_s1 = [Res("s1_%d" % l) for l in range(2)]
        self.normg = self.sb("normg", [128, 2 * 3 * KC], F32)
        self.r_normg = Res("normg")
        self.P.dma("sp", lambda e: e.dma_start(out=self.normg[:], in_=self.norm_gT), writes=[self.r_normg],
                   semkey=self.key("c"))
        self.bmod = self.sb("bmod", [128, 2, NMOD * KC], F32)
        self.r_bmod = Res("bmod")
        self.P.dma("sp", lambda e: e.dma_start(out=self.bmod[:], in_=self.b_modT.rearrange("l p j -> p l j")),
                   writes=[self.r_bmod], semkey=self.key("c"))
        self.psA = [(self.ps("psA%d" % i, [128, 512]), Res("psA%d" % i)) for i in range(4)]
        self.psB = [(self.ps("psB%d" % i, [128, 512]), Res("psB%d" % i)) for i in range(3)]
        self.psM = (self.ps("psM", [128, 512]), Res("psM"))
        self.rotA = Rot(self.psA)
        self.rotB = Rot(self.psB)
        self.aT = self.sb("aT", [128, 2, TB], BF16)
        self.r_aT = [[Res() for t in range(self.NT)] for m in range(2)]
        self.wgu = Rot([(self.sb("wgu%d" % i, [128, KC, 256], BF16), Res("wgu%d" % i), "wgu%d" % i) for i in range(3)])
        self.wd = Rot([(self.sb("wd%d" % i, [128, 2, D], BF16), Res("wd%d" % i), "wd%d" % i) for i in range(2)])
        self.sg = Rot([(self.sb("sg%d" % i, [128, 512], BF16), Res("sg%d" % i)) for i in range(2)])
        self.rstd = self.sb("rstd", [128, 512], F32)
        self.r_rstd = Res("rstd")
        self.sq = Rot([(self.sb("sq%d" % i, [128, 512], BF16), Res("sq%d" % i)) for i in range(2)])
        self.tmpn = Rot([(self.sb("tmpn%d" % i, [128, 512], BF16), Res("tmpn%d" % i)) for i in range(2)])

    def modulation(self, l):
        P = self.P
        if not hasattr(self, 'r_cs'):
            self.cT_sb = self.sb("cT_sb", [128, KC, 2], F32)
            self.cs_bf = self.sb("cs_bf", [128, KC, 2], BF16)
            self.r_cT = Res("cT")
            self.r_cs = Res("cs")
            P.dma("sp", lambda e: e.dma_start(out=self.cT_sb[:], in_=self.cT), writes=[self.r_cT], semkey=self.key("c"))
            P.op("act", lambda e: e.activation(out=self.cs_bf[:], in_=self.cT_sb[:], func=AF.Silu),
                 reads=[self.r_cT], writes=[self.r_cs])
        psm, r_psm = self.psM
        NJ = NMOD * KC
        wv = self.w_mod[l].rearrange("(k p) c -> p k c", p=128)
        for jt in range(NJ // 2):
            wt, r_wt, wkey = self.wgu.next()
            P.dma("pool", lambda e, wt=wt, jt=jt: e.dma_start(out=wt[:], in_=wv[:, :, jt * 256:(jt + 1) * 256]),
                  writes=[r_wt], semkey=wkey)

            def mm(e, wt=wt, jt=jt):
                ins = None
                for jj in range(2):
                    j = jt * 2 + jj
                    for k in range(KC):
                        ins = e.matmul(psm[:, j * 2:j * 2 + 2], wt[:, k, jj * 128:(jj + 1) * 128], self.cs_bf[:, k, :],
                                       start=(k == 0), stop=(k == KC - 1))
                return ins
            P.op("pe", mm, reads=[r_wt, self.r_cs], writes=[r_psm])
        for cv in range(2):
            P.op("dve", lambda e, cv=cv: e.tensor_tensor(
                out=self.modT[:, l, cv, :], in0=psm[:, 0:2 * NJ].rearrange("p (j c) -> p j c", c=2)[:, :, cv],
                in1=self.bmod[:, l, :], op=ALU.add),
                reads=[r_psm, self.r_bmod], writes=[self.r_mod[l]])
        for i in range(3):
            for cv in range(2):
                def f(e, i=i, cv=cv):
                    ins = e.scalar_tensor_tensor(
                        out=self.s1[:, l, i, cv, :], in0=self.modT[:, l, cv, (3 * i + 1) * KC:(3 * i + 2) * KC], scalar=1.0,
                        in1=self.normg[:, (l * 3 + i) * KC:(l * 3 + i + 1) * KC], op0=ALU.add, op1=ALU.mult)
                    return ins
                P.op("dve", f, reads=[self.r_mod[l], self.r_normg], writes=[self.r_s1[l]])

                def g(e, i=i, cv=cv):
                    fac = 1.0 if i == 1 else 0.5
                    return e.tensor_scalar(out=self.gt[:, l, i, cv, :], in0=self.modT[:, l, cv, (3 * i + 2) * KC:(3 * i + 3) * KC],
                                           scalar1=fac, scalar2=None, op0=ALU.mult)
                P.op("dve", g, reads=[self.r_mod[l]], writes=[self.r_s1[l]])

    def shift_ap(self, l, i, cv, k):
        j = (3 * i) * KC + k
        return self.modT[:, l, cv, j:j + 1]

    def cv_of(self, blk, t):
        return 0 if (blk == 0 and t == 0) else 1

    def load_x(self, blk, src):
        P = self.P
        TB = self.TB
        v = src.rearrange("(k p) t -> p k t", p=128)
        g = self.key("g")
        for k in range(KC):
            P.dma("sp", lambda e, k=k: e.dma_start(out=self.xT[:, k, :], in_=v[:, k, blk * TB:(blk + 1) * TB]),
                  writes=self.rx[k], semkey="xload", group=g)

    def store_x(self, blk, dst):
        P = self.P
        TB = self.TB
        v = dst.rearrange("(k p) t -> p k t", p=128)
        g = self.key("g")
        for k in range(KC):
            P.dma("sp", lambda e, k=k: e.dma_start(out=v[:, k, blk * TB:(blk + 1) * TB], in_=self.xT[:, k, :]),
                  reads=self.rx[k], semkey="xstore", group=g)

    def adaln(self, blk, l, i):
        P = self.P
        for t in range(self.NT):
            cv = self.cv_of(blk, t)
            ts = slice(t * 512, (t + 1) * 512)
            pst, r_pst = self.rotB.next()
            for k in range(KC):
                sq, r_sq = self.sq.next()
                P.op("act", lambda e, sq=sq, k=k, ts=ts: e.activation(out=sq[:], in_=self.xT[:, k, ts], func=AF.Square),
                     reads=[self.rx[k][t]], writes=[r_sq])
                P.op("pe", lambda e, sq=sq, k=k, pst=pst: e.matmul(pst[:], self.ones_bf[:], sq[:], start=(k == 0), stop=(k == KC - 1)),
                     reads=[r_sq, self.r_ones], writes=[r_pst])
            P.op("act", lambda e, pst=pst: e.activation(out=self.rstd[:], in_=pst[:], func=AF.Sqrt, scale=1.0 / D, bias=self.eps_ap()),
                 reads=[r_pst, self.r_eps], writes=[self.r_rstd])
            P.op("dve", lambda e: e.reciprocal(out=self.rstd[:], in_=self.rstd[:]), reads=[self.r_rstd], writes=[self.r_rstd])
            for k in range(KC):
                tm, r_tm = self.tmpn.next()
                P.op("dve", lambda e, tm=tm, k=k, ts=ts: e.tensor_tensor(out=tm[:], in0=self.xT[:, k, ts], in1=self.rstd[:], op=ALU.mult),
                     reads=[self.rx[k][t], self.r_rstd], writes=[r_tm])
                P.op("act", lambda e, tm=tm, k=k, ts=ts, cv=cv: e.activation(
                    out=self.hT[:, k, ts], in_=tm[:], func=AF.Identity,
                    scale=self.s1[:, l, i, cv, k:k + 1], bias=self.shift_ap(l, i, cv, k)),
                    reads=[r_tm, self.r_s1[l], self.r_mod[l]], writes=[self.rh[k][t]])

    def eps_ap(self):
        return self.eps_t[:, 0:1]

    def alloc_eps(self):
        self.eps_t = self.sb("eps_t", [128, 1], F32)
        self.r_eps = Res("eps")
        self.P.op("pool", lambda e: e.memset(self.eps_t[:], EPS), writes=[self.r_eps])

    def ffn(self, blk, l, half):
        P = self.P
        i = 0 if half == 0 else 2
        NT = self.NT
        wg_v = self.ffn_wg[l, half].rearrange("(k p) c -> p k c", p=128)
        wu_v = self.ffn_wu[l, half].rearrange("(k p) c -> p k c", p=128)
        wd_v = self.ffn_wd[l, half].rearrange("(m p) n -> p m n", p=128)
        for part in range(MC // 2):
            cs = slice(part * 256, (part + 1) * 256)
            wg, r_wg, kg = self.wgu.next()
            P.dma("pool", lambda e, wg=wg, cs=cs: e.dma_start(out=wg[:], in_=wg_v[:, :, cs]), writes=[r_wg], semkey=kg)
            wu, r_wu, ku = self.wgu.next()
            P.dma("pool", lambda e, wu=wu, cs=cs: e.dma_start(out=wu[:], in_=wu_v[:, :, cs]), writes=[r_wu], semkey=ku)
            wd, r_wd, kd = self.wd.next()
            P.dma("pool", lambda e, wd=wd, part=part: e.dma_start(out=wd[:], in_=wd_v[:, part * 2:part * 2 + 2, :]),
                  writes=[r_wd], semkey=kd)
            for m in range(2):
                for t in range(NT):
                    ts = slice(t * 512, (t + 1) * 512)
                    pg, r_pg = self.rotA.next()
                    pu, r_pu = self.rotA.next()

                    def mm(e, w=wg, pp=pg, m=m, ts=ts):
                        ins = None
                        for k in range(KC):
                            ins = e.matmul(pp[:], w[:, k, m * 128:(m + 1) * 128], self.hT[:, k, ts], start=(k == 0), stop=(k == KC - 1))
                        return ins
                    P.op("pe", mm, reads=[r_wg] + [self.rh[k][t] for k in range(KC)], writes=[r_pg])
                    P.op("pe", lambda e, mm=mm, wu=wu, pu=pu: mm(e, w=wu, pp=pu), reads=[r_wu] + [self.rh[k][t] for k in range(KC)], writes=[r_pu])
                    sg, r_sg = self.sg.next()
                    P.op("act", lambda e, sg=sg, pg=pg: e.activation(out=sg[:], in_=pg[:], func=AF.Silu), reads=[r_pg], writes=[r_sg])
                    P.op("dve", lambda e, sg=sg, pu=pu, m=m, ts=ts: e.tensor_tensor(out=self.aT[:, m, ts], in0=sg[:], in1=pu[:], op=ALU.mult),
                         reads=[r_sg, r_pu], writes=[self.r_aT[m][t]])
            for n in range(KC):
                for t in range(NT):
                    cv = self.cv_of(blk, t)
                    ts = slice(t * 512, (t + 1) * 512)
                    py, r_py = self.rotB.next()

                    def mmd(e, py=py, n=n, ts=ts, wd=wd):
                        ins = None
                        for m in range(2):
                            ins = e.matmul(py[:], wd[:, m, n * 128:(n + 1) * 128], self.aT[:, m, ts], start=(m == 0), stop=(m == 1))
                        return ins
                    P.op("pe", mmd, reads=[r_wd, self.r_aT[0][t], self.r_aT[1][t]], writes=[r_py])
                    P.op("dve", lambda e, py=py, n=n, ts=ts, cv=cv: e.scalar_tensor_tensor(
                        out=self.xT[:, n, ts], in0=py[:], scalar=self.gt[:, l, i, cv, n:n + 1], in1=self.xT[:, n, ts],
                        op0=ALU.mult, op1=ALU.add),
                        reads=[r_py, self.r_s1[l], self.rx[n][t]], writes=[self.rx[n][t]])

    def arena_reset(self):
        self.a32 = self.xT[:].rearrange("p a b -> p (a b)")
        self.a16 = self.hT[:].rearrange("p a b -> p (a b)")
        self.a32_off = 0
        self.a16_off = 0

    def t32(self, n, shape=None, parts=128):
        n4 = (n + 3) // 4 * 4
        v = self.a32[0:parts, self.a32_off:self.a32_off + n]
        self.a32_off += n4
        assert self.a32_off <= 24576, self.a32_off
        return v, Res()

    def t16(self, n, parts=128):
        n4 = (n + 7) // 8 * 8
        v = self.a16[0:parts, self.a16_off:self.a16_off + n]
        self.a16_off += n4
        assert self.a16_off <= 24576, self.a16_off
        return v, Res()

    def declare_io2(self):
        T = self.T
        self.w_in = self.dram_in("w_in_ab", [1, D, 6192])
        self.w_out = self.dram_in("w_out_ab", [1, D, D])
        self.w_qkv = self.dram_in("w_qkv", [1, D, 3072])
        self.w_o = self.dram_in("w_o", [1, D, D])
        self.consts = self.dram_in("consts", [128, 768])
        self.convT = self.dram_in("convT", [128, 4, 8])
        self.mbias = self.dram_in("mbias", [4, 4])
        self.outgT = self.dram_in("outgT", [128, 16])
        self.gkwb = self.dram_in("gkwb", [17, 2, 512])
        self.qkg = self.dram_in("qkg", [128, 2])
        self.qkg_row = self.dram_in("qkg_row", [2, 128])
        self.st_C = self.dram_in("st_C", [2, 4, 128, 256])
        self.st_n = self.dram_in("st_n", [2, 4, 128])
        self.st_m = self.dram_in("st_m", [2, 4])
        self.st_S = self.dram_in("st_S", [2, 4, 128, 256])
        self.cacheKT = self.dram_in("cacheKT", [4, 128, 256])
        self.cacheV = self.dram_in("cacheV", [256, 4, 128])
        self.ropeT = self.dram_in("ropeT", [2, 128, 4096])
        self.o_C = self.dram_out("o_C", [2, 2, 4, 128, 256])
        self.o_n = self.dram_out("o_n", [2, 2, 4, 128])
        self.o_m = self.dram_out("o_m", [2, 2, 4])
        self.o_S = self.dram_out("o_S", [2, 2, 4, 128, 256])
        self.o_k = self.dram_out("o_k", [512, 512])
        self.o_v = self.dram_out("o_v", [512, 512])
        self.XS = self.dram_tmp("XS", [D, T])
        self.PF = self.dram_tmp("PF", [16, 128, T])
        self.PT = self.dram_tmp("PT", [T, 4096])
        self.GRa = self.dram_tmp("GRa", [4, 4, T])
        self.GRb = self.dram_tmp("GRb", [2, 16, T])
        self.HMD = self.dram_tmp("HMD", [T, 8, 256])
        self.HMB = self.dram_tmp("HMB", [T, 8, 256])
        self.YT = self.dram_tmp("YT", [16, 128, T], BF16)
        self.QT = self.dram_tmp("QT", [16, 128, T], BF16)
        self.KT = self.dram_tmp("KT", [4, 128, T], BF16)
        self.VT = self.dram_tmp("VT", [T, 512], BF16)
        self.OT = self.dram_tmp("OT", [16, 128, T], BF16)

    def alloc_consts(self):
        P = self.P
        self.cst = self.sb("cst", [128, 768], F32)
        self.r_cst = Res("cst")
        P.dma("sp", lambda e: e.dma_start(out=self.cst[:], in_=self.consts), writes=[self.r_cst], semkey=self.key("c"))
        self.identF = self.cst[:, 0:128]
        self.maskF = [self.cst[:, 128:256], self.cst[:, 256:384]]
        self.onesF = self.cst[:, 384:512]
        self.ropeR = self.cst[:, 512:640]
        self.identB = self.sb("identB", [128, 128], BF16)
        self.r_identB = Res("identB")
        P.op("dve", lambda e: e.tensor_copy(self.identB[:], self.identF), reads=[self.r_cst], writes=[self.r_identB])
        self.sm = self.sb("smallc", [128, 64], F32)
        self.r_sm = Res("sm")
        P.dma("sp", lambda e: e.dma_start(out=self.sm[:, 0:32], in_=self.convT.rearrange("p a b -> p (a b)")), writes=[self.r_sm], semkey=self.key("c"))
        P.dma("sp", lambda e: e.dma_start(out=self.sm[:, 32:48], in_=self.outgT), writes=[self.r_sm], semkey=self.key("c"))
        P.dma("sp", lambda e: e.dma_start(out=self.sm[:, 48:50], in_=self.qkg), writes=[self.r_sm], semkey=self.key("c"))
        self.mb = self.sb("mb", [4, 8], F32)
        self.r_mb = Res("mb")
        P.dma("sp", lambda e: e.dma_start(out=self.mb[:, 0:4], in_=self.mbias), writes=[self.r_mb], semkey=self.key("c"))
        P.op("dve", lambda e: e.tensor_scalar(out=self.mb[:, 4:8], in0=self.mb[:, 0:4], scalar1=-1.0, scalar2=None, op0=ALU.mult),
             reads=[self.r_mb], writes=[self.r_mb])

    def convw(self, tap, idx):
        return self.sm[:, tap * 8 + idx:tap * 8 + idx + 1]

    def stage_tiles(self):
        fl = self.aT[:].rearrange("p a b -> p (a b)").bitcast(F32)
        cells = [self.r_aT[m][t] for m in range(2) for t in range(self.NT)]
        out = []
        for j in range(self.NT):
            out.append((fl[:, j * 512:(j + 1) * 512], [cells[2 * j], cells[2 * j + 1]], "stg%d" % j))
        return Rot(out)

    def proj_fm(self, wv, c0, ncols, dst_fn, post=None):
        P = self.P
        nch = ncols // 128
        for c in range(0, nch, 2):
            w = min(2, nch - c)
            wt, r_wt, wkey = self.wgu.next()
            P.dma("pool", lambda e, wt=wt, c=c, w=w: e.dma_start(out=wt[:, :, 0:w * 128], in_=wv[:, :, c0 + c * 128:c0 + (c + w) * 128]),
                  writes=[r_wt], semkey=wkey)
            for cc in range(w):
                for t in range(self.NT):
                    ts = slice(t * 512, (t + 1) * 512)
                    ps, r_ps = self.rotA.next()

                    def mm(e, wt=wt, cc=cc, ts=ts, ps=ps):
                        ins = None
                        for k in range(KC):
                            ins = e.matmul(ps[:], wt[:, k, cc * 128:(cc + 1) * 128], self.hT[:, k, ts], start=(k == 0), stop=(k == KC - 1))
                        return ins
                    P.op("pe", mm, reads=[r_wt] + [self.rh[k][t] for k in range(KC)], writes=[r_ps])
                    dst_fn(c + cc, t, ps, r_ps)

    def proj_tm(self, wv, c0, dst_fn, tgs=None):
        P = self.P
        wa, r_wa, ka = self.wgu.next()
        P.dma("pool", lambda e: e.dma_start(out=wa[:], in_=wv[:, :, c0:c0 + 256]), writes=[r_wa], semkey=ka)
        wb, r_wb, kb = self.wgu.next()
        P.dma("pool", lambda e: e.dma_start(out=wb[:], in_=wv[:, :, c0 + 256:c0 + 512]), writes=[r_wb], semkey=kb)
        for tg in (tgs if tgs is not None else range(self.TB // 128)):
            t = tg // 4
            ps, r_ps = self.rotA.next()

            def mm(e, tg=tg, ps=ps):
                ins = None
                for (w, off) in ((wa, 0), (wb, 256)):
                    for k in range(KC):
                        ins = e.matmul(ps[:, off:off + 256], self.hT[:, k, tg * 128:(tg + 1) * 128], w[:, k, :], start=(k == 0), stop=(k == KC - 1))
                return ins
            P.op("pe", mm, reads=[r_wa, r_wb] + [self.rh[k][t] for k in range(KC)], writes=[r_ps])
            dst_fn(tg, ps, r_ps)

    def inproj(self, blk):
        P = self.P
        tok0 = blk * self.TB
        wv = self.w_in[0].rearrange("(k p) c -> p k c", p=128)
        stg = self.stage_tiles()
        for gi, c0 in enumerate((0, 512, 3088, 3600)):
            def dst(ci, t, ps, r_ps, gi=gi):
                st, r_st, kst = stg.next()
                P.op("act", lambda e: e.activation(out=st, in_=ps[:], func=AF.Copy), reads=[r_ps], writes=r_st)
                idx = gi * 4 + ci
                P.dma("sp", lambda e: e.dma_start(out=self.PF[idx, :, tok0 + t * 512:tok0 + (t + 1) * 512], in_=st), reads=r_st, semkey=kst)
            self.proj_fm(wv, c0, 512, dst)
        for gi, cs in enumerate((1024, 2048, 4112, 5136)):
            for half in range(2):
                def dst(tg, ps, r_ps, gi=gi, half=half):
                    st, r_st, kst = stg.next()
                    P.op("act", lambda e: e.activation(out=st, in_=ps[:], func=AF.Copy), reads=[r_ps], writes=r_st)
                    P.dma("sp", lambda e: e.dma_start(
                        out=self.PT[tok0 + tg * 128:tok0 + (tg + 1) * 128, gi * 1024 + half * 512:gi * 1024 + (half + 1) * 512], in_=st),
                        reads=r_st, semkey=kst)
                self.proj_tm(wv, cs + half * 512, dst)
        wt, r_wt, wkey = self.wgu.next()
        P.dma("pool", lambda e: e.dma_start(out=wt[:, :, 0:16], in_=wv[:, :, 3072:3088]), writes=[r_wt], semkey=wkey)
        P.dma("pool", lambda e: e.dma_start(out=wt[:, :, 16:48], in_=wv[:, :, 6160:6192]), writes=[r_wt], semkey=wkey)
        groups = [(4 * g, 4, self.GRa[g]) for g in range(4)] + [(16 + 16 * j, 16, self.GRb[j]) for j in range(2)]
        for (co, n, dstap) in groups:
            for t in range(self.NT):
                ts = slice(t * 512, (t + 1) * 512)
                ps, r_ps = self.rotB.next()

                def mm(e, co=co, n=n, ts=ts, ps=ps):
                    ins = None
                    for k in range(KC):
                        ins = e.matmul(ps[0:n, :], wt[:, k, co:co + n], self.hT[:, k, ts], start=(k == 0), stop=(k == KC - 1))
                    return ins
                P.op("pe", mm, reads=[r_wt] + [self.rh[k][t] for k in range(KC)], writes=[r_ps])
                st, r_st, kst = stg.next()
                P.op("act", lambda e, st=st, ps=ps, n=n: e.activation(out=st[0:n, :], in_=ps[0:n, :], func=AF.Copy), reads=[r_ps], writes=r_st)
                P.dma("sp", lambda e, st=st, n=n, dstap=dstap, t=t: e.dma_start(out=dstap[:, tok0 + t * 512:tok0 + (t + 1) * 512], in_=st[0:n, :]),
                      reads=r_st, semkey=kst)

    def mixer_core(self, segs):
        P = self.P
        P.barrier()
        self.arena_reset()
        SM = max(s[1] for s in segs)
        NCM = SM // 128
        R0, r_R0 = self.t32(SM, parts=4)
        R1, r_R1 = self.t32(SM, parts=4)
        R2, r_R2 = self.t32(SM, parts=4)
        RST, r_RST = self.t32(SM, parts=4)
        P.op("pool", lambda e: e.memset(RST, 1.0), writes=[r_RST])
        P.op("pool", lambda e: e.memset(RST.rearrange("p (c l) -> p c l", l=128)[:, :, 0:1], 0.0), writes=[r_RST])
        xin, r_xin = self.t32(520)
        acc, r_acc = self.t32(512)
        COLS = [self.t32(NCM * 12) for _ in range(2)]
        CstD = [self.t32(260) for _ in range(2)]
        SstD = [self.t32(256) for _ in range(2)]
        hrotD = [Rot([self.t32(256) for _ in range(2)]) for _ in range(2)]
        hrotF = Rot([self.t32(256) for _ in range(3)])
        grot = Rot([self.t32(256) for _ in range(2)])
        f128D = [Rot([self.t32(128) for _ in range(6)]) for _ in range(2)]
        lrtD = [Rot([self.t32(128, parts=17)]) for _ in range(2)]
        for (v, r) in lrtD[0].items + lrtD[1].items:
            P.op("pool", lambda e, v=v: e.memset(v, 1.0), writes=[r])
        gk, r_gk = self.t32(1024, parts=17)
        P.dma("sp", lambda e: e.dma_start(out=gk, in_=self.gkwb.rearrange("r d c -> r (d c)")), writes=[r_gk], semkey=self.key("c"))
        AM, r_AM = self.t32(NCM, parts=4)
        MS, r_MS = self.t32(NCM, parts=4)
        WI, r_WI = self.t32(NCM, parts=4)
        mcur, r_mcur = self.t32(4, parts=4)
        wdg, r_wdg = self.t32(4, parts=4)
        smallD = [Rot([self.t32(8) for _ in range(2)]) for _ in range(2)]
        small = Rot([self.t32(8) for _ in range(2)])
        QC, r_QC = self.t16(SM)
        KCc, r_KC = self.t16(SM)
        Ktok, r_Ktok = self.t16(NCM * 128)
        Vx, r_Vx = self.t16(NCM * 257)
        b128D = [Rot([self.t16(128) for _ in range(5)]) for _ in range(2)]
        b128 = Rot([self.t16(128) for _ in range(2)])
        b260D = [Rot([self.t16(260) for _ in range(2)]) for _ in range(2)]
        b256D = [Rot([self.t16(256) for _ in range(1)]) for _ in range(2)]
        b256 = Rot([self.t16(256) for _ in range(2)])
        allps = self.psA + self.psB + [self.psM]
        pss = Rot(allps)
        allap = [(p_[:], r_) for (p_, r_) in allps]
        pssD = [Rot(allap[0:4]), Rot(allap[4:8])]
        halves = []
        for (pt_full, _r) in allps:
            halves.append((pt_full[:, 0:256], Res()))
            halves.append((pt_full[:, 256:512], Res()))
        pshD = [Rot(halves[0:6]), Rot(halves[6:12])]
        pshF = Rot(halves[12:16])
        hkey = Rot(["hmd%d" % i for i in range(3)])
        ykey = Rot(["yt%d" % i for i in range(4)])
        r_HMD = {}
        P.barrier()

        def seg_body(seg0, S, kind):
            NC = S // 128
            is_s = kind[0] == 's'
            mfin = [None, None]
            def prep_body(d):
                CL, r_CL = COLS[d]
                r0 = R0[:, 0:S]
                r1 = R1[:, 0:S]
                r2 = R2[:, 0:S]
                P.dma("sp", lambda e, r0=r0, d=d: e.dma_start(out=r0, in_=self.GRa[2 * d][:, seg0:seg0 + S]), writes=[r_R0], semkey="gr0")
                P.dma("sp", lambda e, r1=r1, d=d: e.dma_start(out=r1, in_=self.GRa[2 * d + 1][:, seg0:seg0 + S]), writes=[r_R1], semkey="gr1")
                P.op("dve", lambda e, r0=r0, d=d: e.tensor_scalar(out=r0, in0=r0, scalar1=self.mb[:, 2 * d:2 * d + 1], scalar2=None, op0=ALU.add),
                     reads=[r_R0, self.r_mb], writes=[r_R0])
                P.op("act", lambda e, r1=r1, d=d: e.activation(out=r1, in_=r1, func=AF.Exp, scale=-1.0, bias=self.mb[:, 4 + 2 * d + 1:4 + 2 * d + 2]),
                     reads=[r_R1, self.r_mb], writes=[r_R1])
                P.op("act", lambda e, r1=r1: e.activation(out=r1, in_=r1, func=AF.Ln, scale=1.0, bias=self.onesF[0:4, 0:1]),
                     reads=[r_R1, self.r_cst], writes=[r_R1])
                P.op("dve", lambda e, r1=r1, r2=r2: e.tensor_tensor_scan(out=r2, data0=RST[:, 0:S], data1=r1, initial=0.0, op0=ALU.mult, op1=ALU.add),
                     reads=[r_R1, r_RST], writes=[r_R2])
                if d == 0:
                    NB, r_NB = r2, r_R2
                else:
                    r13 = r1.rearrange("p (c l) -> p c l", l=128)
                    r23 = r2.rearrange("p (c l) -> p c l", l=128)
                    P.op("dve", lambda e, r1=r1, r2=r2: e.tensor_tensor(out=r1, in0=r1, in1=r2, op=ALU.subtract), reads=[r_R1, r_R2], writes=[r_R1])
                    P.op("dve", lambda e, r13=r13, r23=r23, NC=NC: e.tensor_tensor(out=r13, in0=r13, in1=self.bc_last(r23[:, :, 127:128], 128), op=ALU.add),
                         reads=[r_R1, r_R2], writes=[r_R1])
                    NB, r_NB = r1, r_R1
                NB3 = NB.rearrange("p (c l) -> p c l", l=128)
                r03 = r0.rearrange("p (c l) -> p c l", l=128)
                P.op("dve", lambda e, r0=r0, NB=NB: e.tensor_tensor(out=r0, in0=r0, in1=NB, op=ALU.add), reads=[r_R0, r_NB], writes=[r_R0])
                P.op("dve", lambda e, r03=r03, NC=NC: e.tensor_reduce(out=AM[:, 0:NC], in_=r03, axis=AX.X, op=ALU.max), reads=[r_R0], writes=[r_AM])
                if is_s:
                    P.dma("sp", lambda e, d=d: e.dma_start(out=mcur[:, 0:1], in_=self.st_m[d].rearrange("(h o) -> h o", o=1)), writes=[r_mcur], semkey="mc")
                else:
                    P.op("dve", lambda e: e.memset(mcur[:, 0:1], 0.0), writes=[r_mcur])
                order = list(range(NC)) if d == 0 else list(range(NC - 1, -1, -1))
                endcol = 127 if d == 0 else 0
                for c in order:
                    P.op("dve", lambda e, c=c: e.tensor_tensor(out=MS[:, c:c + 1], in0=mcur[:, 0:1], in1=AM[:, c:c + 1], op=ALU.max),
                         reads=[r_mcur, r_AM], writes=[r_MS])
                    P.op("dve", lambda e, c=c: e.tensor_tensor(out=WI[:, c:c + 1], in0=mcur[:, 0:1], in1=MS[:, c:c + 1], op=ALU.subtract),
                         reads=[r_mcur, r_MS], writes=[r_WI])
                    P.op("dve", lambda e, c=c, NB3=NB3, endcol=endcol: e.tensor_tensor(out=mcur[:, 0:1], in0=MS[:, c:c + 1], in1=NB3[:, c, endcol:endcol + 1], op=ALU.subtract),
                         reads=[r_MS, r_NB], writes=[r_mcur])
                if not is_s:
                    P.dma("sp", lambda e, d=d, q=kind[1]: e.dma_start(out=self.o_m[q, d].rearrange("(h o) -> h o", o=1), in_=mcur[:, 0:1]),
                          reads=[r_mcur], semkey="om")
                P.op("act", lambda e, NC=NC: e.activation(out=WI[:, 0:NC], in_=WI[:, 0:NC], func=AF.Exp), reads=[r_WI], writes=[r_WI])
                msb = self.bc_last(MS[:, 0:NC].rearrange("p (c o) -> p c o", o=1), 128)
                P.op("dve", lambda e, r03=r03, msb=msb: e.tensor_tensor(out=r03, in0=r03, in1=msb, op=ALU.subtract), reads=[r_R0, r_MS], writes=[r_R0])
                P.op("act", lambda e, r0=r0: e.activation(out=r0, in_=r0, func=AF.Exp), reads=[r_R0], writes=[r_R0])
                P.op("dve", lambda e, NB3=NB3, msb=msb: e.tensor_tensor(out=NB3, in0=NB3, in1=msb, op=ALU.subtract), reads=[r_NB, r_MS], writes=[r_NB])
                P.op("act", lambda e, NB=NB: e.activation(out=NB, in_=NB, func=AF.Exp), reads=[r_NB], writes=[r_NB])
                for c in range(NC):
                    ps, r_ps = pss.next()
                    P.op("dve", lambda e, c=c: e.tensor_scalar(out=wdg[:, 0:4], in0=self.identF[0:4, 0:4], scalar1=WI[:, c:c + 1], scalar2=None, op0=ALU.mult),
                         reads=[r_WI, self.r_cst], writes=[r_wdg])

                    def pe3(e, c=c, ps=ps, r0=r0, NB=NB):
                        e.transpose(ps[:, 0:4], r0[:, c * 128:(c + 1) * 128], self.identF[0:4, 0:4])
                        e.transpose(ps[:, 4:8], NB[:, c * 128:(c + 1) * 128], self.identF[0:4, 0:4])
                        return e.matmul(ps[:, 8:12], self.onesF[0:4, :], wdg[:, 0:4], start=True, stop=True)
                    P.op("pe", pe3, reads=[r_R0, r_NB, r_wdg, self.r_cst], writes=[r_ps])
                    P.op("dve", lambda e, c=c, ps=ps, CL=CL: e.tensor_copy(CL[:, c * 12:(c + 1) * 12], ps[:, 0:12]), reads=[r_ps], writes=[r_CL])
            for _d in range(2):
                prep_body(_d)

            def head_body(hh):
                gla = hh >= 4
                h = hh % 4
                if not gla:
                    for which, dstb, r_dst in ((0, QC, r_QC), (1, KCc, r_KC)):
                        idx = which * 4 + h
                        for p0 in range(0, S, 512):
                            n = min(512, S - p0)
                            lo = 1 if p0 == 0 else 0
                            hi = 1 if p0 + n == S else 0
                            if lo:
                                P.op("dve", lambda e: e.memset(xin[:, 0:1], 0.0), writes=[r_xin])
                            if hi:
                                P.op("dve", lambda e, n=n: e.memset(xin[:, n + 1:n + 2], 0.0), writes=[r_xin])
                            P.dma("sp", lambda e, idx=idx, p0=p0, n=n, lo=lo, hi=hi: e.dma_start(
                                out=xin[:, lo:n + 2 - hi], in_=self.PF[idx, :, seg0 + p0 - 1 + lo:seg0 + p0 + n + 1 - hi]), writes=[r_xin], semkey="xin")
                            P.op("dve", lambda e, n=n, idx=idx: e.tensor_scalar(out=acc[:, 0:n], in0=xin[:, 1:n + 1], scalar1=self.convw(1, idx), scalar2=self.convw(3, idx),
                                                                              op0=ALU.mult, op1=ALU.add), reads=[r_xin, self.r_sm], writes=[r_acc])
                            P.op("dve", lambda e, n=n, idx=idx: e.scalar_tensor_tensor(out=acc[:, 0:n], in0=xin[:, 0:n], scalar=self.convw(0, idx), in1=acc[:, 0:n],
                                                                                      op0=ALU.mult, op1=ALU.add), reads=[r_xin, r_acc, self.r_sm], writes=[r_acc])
                            P.op("dve", lambda e, n=n, idx=idx: e.scalar_tensor_tensor(out=acc[:, 0:n], in0=xin[:, 2:n + 2], scalar=self.convw(2, idx), in1=acc[:, 0:n],
                                                                                      op0=ALU.mult, op1=ALU.add), reads=[r_xin, r_acc, self.r_sm], writes=[r_acc])
                            if which == 0:
                                P.op("act", lambda e, n=n, p0=p0, dstb=dstb: e.activation(out=dstb[:, p0:p0 + n], in_=acc[:, 0:n], func=AF.Silu),
                                     reads=[r_acc], writes=[r_dst])
                            else:
                                P.op("act", lambda e, n=n: e.activation(out=acc[:, 0:n], in_=acc[:, 0:n], func=AF.Silu), reads=[r_acc], writes=[r_acc])
                                P.op("dve", lambda e, n=n, p0=p0, dstb=dstb: e.tensor_scalar(out=dstb[:, p0:p0 + n], in0=acc[:, 0:n], scalar1=128.0 ** -0.5, scalar2=None,
                                                                                         op0=ALU.mult), reads=[r_acc], writes=[r_dst])
                    for c in range(NC):
                        ps, r_ps = pss.next()
                        psb = ps[:].bitcast(BF16)
                        P.op("pe", lambda e, c=c, psb=psb: e.transpose(psb[:, 0:128], KCc[:, c * 128:(c + 1) * 128], self.identB[:]),
                             reads=[r_KC, self.r_identB], writes=[r_ps])
                        P.op("act", lambda e, c=c, psb=psb: e.activation(out=Ktok[:, c * 128:(c + 1) * 128], in_=psb[:, 0:128], func=AF.Copy),
                             reads=[r_ps], writes=[r_Ktok])
                    Vx3 = Vx[:, 0:NC * 257].rearrange("p (c e) -> p c e", e=257)
                    P.op("pool", lambda e, Vx3=Vx3: e.memset(Vx3[:, :, 256:257], 1.0), writes=[r_Vx])
                    for c8 in range(0, NC, 8):
                        ce = min(NC, c8 + 8)
                        P.dma("pool", lambda e, Vx3=Vx3, h=h, c8=c8, ce=ce: e.dma_start(
                            out=Vx3[:, c8:ce, 0:256], in_=self.PT[seg0 + c8 * 128:seg0 + ce * 128, h * 256:(h + 1) * 256].rearrange("(c p) e -> p c e", p=128)),
                            writes=[r_Vx], semkey="vx")
                else:
                    Vg3 = Vx[:, 0:NC * 256].rearrange("p (c e) -> p c e", e=256)
                    for c8 in range(0, NC, 8):
                        ce = min(NC, c8 + 8)
                        P.dma("pool", lambda e, Vg3=Vg3, h=h, c8=c8, ce=ce: e.dma_start(
                            out=Vg3[:, c8:ce, :], in_=self.PT[seg0 + c8 * 128:seg0 + ce * 128, 2048 + h * 256:2048 + (h + 1) * 256].rearrange("(c p) e -> p c e", p=128)),
                            writes=[r_Vx], semkey="vx")

                def dir_body(d):
                    CL, r_CL = COLS[d]
                    order = list(range(NC)) if d == 0 else list(range(NC - 1, -1, -1))
                    msk = self.maskF[d]
                    Cst, r_Cst = CstD[d]
                    Sst, r_Sst = SstD[d]
                    pp = pssD[d]
                    f128 = f128D[d]
                    b128_ = b128D[d]
                    b260 = b260D[d]
                    b256_ = b256D[d]
                    hrot = hrotD[d]
                    small_ = smallD[d]
                    lrt = lrtD[d]
                    HD = self.HMD if d == 0 else self.HMB
                    if not gla:
                        if is_s:
                            P.dma("sp", lambda e: e.dma_start(out=Cst[:, 0:256], in_=self.st_C[d, h]), writes=[r_Cst], semkey="cst%d" % d)
                            P.dma("sp", lambda e: e.dma_start(out=Cst[:, 256:257], in_=self.st_n[d, h].rearrange("(p o) -> p o", o=1)), writes=[r_Cst], semkey="cst%d" % d)
                        else:
                            P.op("dve", lambda e: e.memset(Cst[:, 0:257], 0.0), writes=[r_Cst])
                    else:
                        if is_s:
                            P.dma("sp", lambda e: e.dma_start(out=Sst, in_=self.st_S[d, h]), writes=[r_Sst], semkey="sst%d" % d)
                        else:
                            P.op("dve", lambda e: e.memset(Sst, 0.0), writes=[r_Sst])

                    def chunk_body(c):
                        tc = slice(c * 128, (c + 1) * 128)
                        g0 = seg0 + c * 128
                        rk = r_HMD.setdefault((d, hh, g0), Res())
                        if not gla:
                            ps1, r_ps1 = pp.next()
                            P.op("pe", lambda e: e.matmul(ps1[:, 0:128], KCc[:, tc], QC[:, tc], start=True, stop=True),
                                 reads=[r_KC, r_QC], writes=[r_ps1])
                            stm, r_stm = b128_.next()
                            P.op("dve", lambda e: e.tensor_tensor(out=stm, in0=ps1[:, 0:128], in1=msk, op=ALU.mult),
                                 reads=[r_ps1, self.r_cst], writes=[r_stm])
                            vw, r_vw = b260.next()
                            P.op("act", lambda e: e.activation(out=vw[:, 0:257], in_=Vx[:, c * 257:(c + 1) * 257], func=AF.Copy, scale=CL[:, c * 12 + h:c * 12 + h + 1]),
                                 reads=[r_Vx, r_CL], writes=[r_vw])
                            P.op("dve", lambda e: e.tensor_scalar(out=Cst[:, 0:257], in0=Cst[:, 0:257], scalar1=CL[:, c * 12 + 8 + h:c * 12 + 9 + h], scalar2=None, op0=ALU.mult),
                                 reads=[r_Cst, r_CL], writes=[r_Cst])
                            cb, r_cb = b260.next()
                            P.op("act", lambda e: e.activation(out=cb[:, 0:257], in_=Cst[:, 0:257], func=AF.Copy), reads=[r_Cst], writes=[r_cb])
                            pso, r_pso = pp.next()

                            def mo(e):
                                e.matmul(pso[:, 0:257], QC[:, tc], cb[:, 0:257], start=True, stop=False)
                                return e.matmul(pso[:, 0:257], stm, vw[:, 0:257], start=False, stop=True)
                            P.op("pe", mo, reads=[r_QC, r_cb, r_stm, r_vw], writes=[r_pso])
                            psc, r_psc = pp.next()
                            P.op("pe", lambda e: e.matmul(psc[:, 0:257], Ktok[:, tc], vw[:, 0:257], start=True, stop=True),
                                 reads=[r_Ktok, r_vw], writes=[r_psc])
                            P.op("dve", lambda e: e.tensor_tensor(out=Cst[:, 0:257], in0=psc[:, 0:257], in1=Cst[:, 0:257], op=ALU.add),
                                 reads=[r_psc, r_Cst], writes=[r_Cst])
                            dn, r_dn = small_.next()
                            P.op("act", lambda e: e.activation(out=dn[:, 0:1], in_=pso[:, 256:257], func=AF.Abs), reads=[r_pso], writes=[r_dn])
                            P.op("dve", lambda e: e.tensor_tensor(out=dn[:, 0:1], in0=dn[:, 0:1], in1=CL[:, c * 12 + 4 + h:c * 12 + 5 + h], op=ALU.max),
                                 reads=[r_dn, r_CL], writes=[r_dn])
                            P.op("dve", lambda e: e.reciprocal(out=dn[:, 0:1], in_=dn[:, 0:1]), reads=[r_dn], writes=[r_dn])
                            ho, r_ho = hrot.next()
                            P.op("dve", lambda e: e.tensor_scalar(out=ho, in0=pso[:, 0:256], scalar1=dn[:, 0:1], scalar2=None, op0=ALU.mult),
                                 reads=[r_pso, r_dn], writes=[r_ho])
                        else:
                            lr, r_lr = lrt.next()
                            P.dma("sp", lambda e: e.dma_start(out=lr[0:16, :], in_=self.GRb[d][:, g0:g0 + 128]), writes=[r_lr], semkey=self.lrkey(lr))
                            psg, r_psg = pp.next()
                            P.op("pe", lambda e: e.matmul(psg[:, 0:128], lr, gk[:, d * 512 + h * 128:d * 512 + (h + 1) * 128], start=True, stop=True),
                                 reads=[r_lr, r_gk], writes=[r_psg])
                            nla, r_nla = f128.next()
                            P.op("act", lambda e: e.activation(out=nla, in_=psg[:, 0:128], func=AF.Exp, scale=-1.0), reads=[r_psg], writes=[r_nla])
                            P.op("act", lambda e: e.activation(out=nla, in_=nla, func=AF.Ln, scale=1.0, bias=self.onesF[:, 0:1]), reads=[r_nla, self.r_cst], writes=[r_nla])
                            psb_, r_psb = pp.next()
                            P.op("pe", lambda e: e.matmul(psb_[:, 0:128], nla, msk, start=True, stop=True),
                                 reads=[r_nla, self.r_cst], writes=[r_psb])
                            eq, r_eq = f128.next()
                            ek, r_ek = f128.next()
                            P.op("act", lambda e: e.activation(out=eq, in_=psb_[:, 0:128], func=AF.Exp, scale=-1.0 / 16.0), reads=[r_psb], writes=[r_eq])
                            P.op("act", lambda e: e.activation(out=ek, in_=psb_[:, 0:128], func=AF.Exp, scale=1.0 / 16.0), reads=[r_psb], writes=[r_ek])
                            qr, r_qr = f128.next()
                            kr, r_kr = f128.next()
                            P.dma("sp", lambda e: e.dma_start(out=qr, in_=self.PF[8 + h, :, g0:g0 + 128]), writes=[r_qr], semkey=self.lrkey(qr))
                            P.dma("sp", lambda e: e.dma_start(out=kr, in_=self.PF[12 + h, :, g0:g0 + 128]), writes=[r_kr], semkey=self.lrkey(kr))
                            qt, r_qt = b128_.next()
                            kt, r_kt = b128_.next()
                            P.op("dve", lambda e: e.scalar_tensor_tensor(out=qt, in0=qr, scalar=128.0 ** -0.5, in1=eq, op0=ALU.mult, op1=ALU.mult),
                                 reads=[r_qr, r_eq], writes=[r_qt])
                            P.op("dve", lambda e: e.tensor_tensor(out=kt, in0=kr, in1=ek, op=ALU.mult), reads=[r_kr, r_ek], writes=[r_kt])
                            pst, r_pst = pp.next()
                            pstb = pst.bitcast(BF16)
                            P.op("pe", lambda e: e.transpose(pstb[:, 0:128], kt, self.identB[:]), reads=[r_kt, self.r_identB], writes=[r_pst])
                            ktk, r_ktk = b128_.next()
                            P.op("act", lambda e: e.activation(out=ktk, in_=pstb[:, 0:128], func=AF.Copy), reads=[r_pst], writes=[r_ktk])
                            psa, r_psa = pp.next()
                            P.op("pe", lambda e: e.matmul(psa[:, 0:128], kt, qt, start=True, stop=True), reads=[r_kt, r_qt], writes=[r_psa])
                            am, r_am = b128_.next()
                            P.op("dve", lambda e: e.tensor_tensor(out=am, in0=psa[:, 0:128], in1=msk, op=ALU.mult), reads=[r_psa, self.r_cst], writes=[r_am])
                            sb_, r_sb = b256_.next()
                            P.op("act", lambda e: e.activation(out=sb_, in_=Sst, func=AF.Copy), reads=[r_Sst], writes=[r_sb])
                            pso, r_pso = pp.next()
                            vc = Vx[:, c * 256:(c + 1) * 256]

                            def mo(e):
                                e.matmul(pso[:, 0:256], qt, sb_, start=True, stop=False)
                                return e.matmul(pso[:, 0:256], am, vc, start=False, stop=True)
                            P.op("pe", mo, reads=[r_qt, r_sb, r_am, r_Vx], writes=[r_pso])
                            pss2, r_pss2 = pp.next()
                            P.op("pe", lambda e: e.matmul(pss2[:, 0:256], ktk, vc, start=True, stop=True), reads=[r_ktk, r_Vx], writes=[r_pss2])
                            P.op("dve", lambda e: e.tensor_tensor(out=Sst, in0=pss2[:, 0:256], in1=Sst, op=ALU.add), reads=[r_pss2, r_Sst], writes=[r_Sst])
                            ec = 127 if d == 0 else 0
                            P.op("dve", lambda e: e.tensor_scalar(out=Sst, in0=Sst, scalar1=eq[:, ec:ec + 1], scalar2=None, op0=ALU.mult),
                                 reads=[r_Sst, r_eq], writes=[r_Sst])
                            ho, r_ho = hrot.next()
                            P.op("act", lambda e: e.activation(out=ho, in_=pso[:, 0:256], func=AF.Copy), reads=[r_pso], writes=[r_ho])
                        P.dma("pool", lambda e: e.dma_start(out=HD[g0:g0 + 128, hh, :], in_=ho), reads=[r_ho], writes=[rk], semkey=self.lrkey(ho))
                    for _c in order:
                        chunk_body(_c)
                    if not is_s:
                        q = kind[1]
                        if not gla:
                            P.dma("sp", lambda e: e.dma_start(out=self.o_C[q, d, h], in_=Cst[:, 0:256]), reads=[r_Cst], semkey="ocs")
                            P.dma("sp", lambda e: e.dma_start(out=self.o_n[q, d, h].rearrange("(p o) -> p o", o=1), in_=Cst[:, 256:257]), reads=[r_Cst], semkey="ocs")
                        else:
                            P.dma("sp", lambda e: e.dma_start(out=self.o_S[q, d, h], in_=Sst), reads=[r_Sst], semkey="oss")

                def fin_body(c):
                    g0 = seg0 + c * 128
                    rk0 = r_HMD[(0, hh, g0)]
                    rk1 = r_HMD[(1, hh, g0)]
                    ho, r_ho = hrotF.next()
                    hf, r_hf = hrotF.next()
                    P.dma("sp", lambda e: e.dma_start(out=ho, in_=self.HMD[g0:g0 + 128, hh, :]), reads=[rk0], writes=[r_ho], semkey=self.lrkey(ho))
                    P.dma("sp", lambda e: e.dma_start(out=hf, in_=self.HMB[g0:g0 + 128, hh, :]), reads=[rk1], writes=[r_hf], semkey=self.lrkey(hf))
                    P.op("dve", lambda e: e.tensor_tensor(out=ho, in0=ho, in1=hf, op=ALU.add), reads=[r_ho, r_hf], writes=[r_ho])
                    ss, r_ss = small.next()
                    P.op("act", lambda e: e.activation(out=hf, in_=ho, func=AF.Square, accum_out=ss[:, 0:1]), reads=[r_ho], writes=[r_hf, r_ss])
                    P.op("act", lambda e: e.activation(out=ss[:, 0:1], in_=ss[:, 0:1], func=AF.Sqrt, scale=1.0 / 256.0, bias=self.eps_ap()), reads=[r_ss, self.r_eps], writes=[r_ss])
                    P.op("dve", lambda e: e.reciprocal(out=ss[:, 0:1], in_=ss[:, 0:1]), reads=[r_ss], writes=[r_ss])
                    gt_, r_gt = grot.next()
                    gcol = (1024 if not gla else 3072) + h * 256
                    P.dma("sp", lambda e: e.dma_start(out=gt_, in_=self.PT[g0:g0 + 128, gcol:gcol + 256]), writes=[r_gt], semkey=self.lrkey(gt_))
                    P.op("act", lambda e: e.activation(out=gt_, in_=gt_, func=(AF.Silu if gla else AF.Sigmoid)), reads=[r_gt], writes=[r_gt])
                    yb, r_yb = b256.next()
                    P.op("dve", lambda e: e.scalar_tensor_tensor(out=yb, in0=ho, scalar=ss[:, 0:1], in1=gt_, op0=ALU.mult, op1=ALU.mult),
                         reads=[r_ho, r_ss, r_gt], writes=[r_yb])

                    def tr(j):
                        pt_, r_pt = pss.next()
                        ptb = pt_[:].bitcast(BF16)
                        P.op("pe", lambda e: e.transpose(ptb[:, 0:128], yb[:, j * 128:(j + 1) * 128], self.identB[:]),
                             reads=[r_yb, self.r_identB], writes=[r_pt])
                        yt, r_yt = b128.next()
                        ci = hh * 2 + j
                        P.op("dve", lambda e: e.tensor_scalar(out=yt, in0=ptb[:, 0:128], scalar1=self.sm[:, 32 + ci:33 + ci], scalar2=None, op0=ALU.mult),
                             reads=[r_pt, self.r_sm], writes=[r_yt])
                        P.dma("pool", lambda e: e.dma_start(out=self.YT[ci, :, g0:g0 + 128], in_=yt), reads=[r_yt], semkey=self.lrkey(yt))
                    tr(0)
                    tr(1)
                chains = []
                for _d in range(2):
                    P.capture = []
                    dir_body(_d)
                    chains.append(P.capture)
                    P.capture = None
                P.replay_interleaved(chains)
                for _c in range(NC):
                    fin_body(_c)
                if hh == 3 or hh == 7:
                    P.barrier()
            for _hh in getattr(self, 'heads', range(8)):
                head_body(_hh)
        for _sg in segs:
            seg_body(*_sg)
        P.barrier()

    @staticmethod
    def bc_last(v, n):
        a = v.ap
        return bass.AP(v.tensor, v.offset, [list(a[0]), list(a[1]), [0, n]])

    def lrkey(self, v):
        k = id(v)
        if not hasattr(self, "_lrk"):
            self._lrk = {}
        if k not in self._lrk:
            self._lrk[k] = "tk%d" % len(self._lrk)
            self._lrv = getattr(self, "_lrv", [])
            self._lrv.append(v)
        return self._lrk[k]

    def build_test_mixer(self):
        self.declare_io()
        self.declare_io2()
        self.alloc_eps()
        self.alloc_common()
        self.alloc_consts()
        self.modulation(0)
        for blk in range(self.NBLK):
            self.load_x(blk, self.xT_in)
            self.adaln(blk, 0, 1)
            self.inproj(blk)
        if not getattr(self, 'skip_core', False):
            self.mixer_core(self.segs)
        self.dbgY = self.dram_out("dbgY", [16, 128, self.T], BF16)
        stg, r_stg = self.hT[:, 0, :], Res()
        TB = self.TB
        for ci in range(16):
            for bk in range(self.NBLK):
                self.P.dma("sp", lambda e, ci=ci, bk=bk: e.dma_start(out=stg[:, 0:TB], in_=self.YT[ci][:, bk * TB:(bk + 1) * TB]), writes=[r_stg], semkey="dbg1")
                self.P.dma("sp", lambda e, ci=ci, bk=bk: e.dma_start(out=self.dbgY[ci][:, bk * TB:(bk + 1) * TB], in_=stg[:, 0:TB]), reads=[r_stg], semkey="dbg2")
        self.P.emit()
        self.stack.close()
        return self.nc

    def outproj(self, blk, wdram, src, l):
        P = self.P
        TB = self.TB
        tok0 = blk * TB
        g = self.key("g")
        for j in range(KC):
            P.dma("sp", lambda e, j=j: e.dma_start(out=self.hT[:, j, :], in_=src[j, :, tok0:tok0 + TB]), writes=self.rh[j], semkey="hload", group=g)
        wv = wdram.rearrange("(k p) c -> p k c", p=128)
        for npair in range(8):
            wt, r_wt, wkey = self.wgu.next()
            P.dma("pool", lambda e, wt=wt, npair=npair: e.dma_start(out=wt[:], in_=wv[:, :, npair * 256:(npair + 1) * 256]), writes=[r_wt], semkey=wkey)
            for cc in range(2):
                n = npair * 2 + cc
                for t in range(self.NT):
                    cv = self.cv_of(blk, t)
                    ts = slice(t * 512, (t + 1) * 512)
                    ps, r_ps = self.rotB.next()

                    def mm(e, wt=wt, cc=cc, ts=ts, ps=ps):
                        ins = None
                        for j in range(KC):
                            ins = e.matmul(ps[:], wt[:, j, cc * 128:(cc + 1) * 128], self.hT[:, j, ts], start=(j == 0), stop=(j == KC - 1))
                        return ins
                    P.op("pe", mm, reads=[r_wt] + [self.rh[j][t] for j in range(KC)], writes=[r_ps])
                    P.op("dve", lambda e, ps=ps, n=n, ts=ts, cv=cv: e.scalar_tensor_tensor(
                        out=self.xT[:, n, ts], in0=ps[:], scalar=self.gt[:, l, 1, cv, n:n + 1], in1=self.xT[:, n, ts], op0=ALU.mult, op1=ALU.add),
                        reads=[r_ps, self.r_s1[l], self.rx[n][t]], writes=[self.rx[n][t]])

    def qkvproj(self, blk):
        P = self.P
        TB = self.TB
        tok0 = blk * TB
        P.barrier()
        wv = self.w_qkv[0].rearrange("(k p) c -> p k c", p=128)
        f0 = self.wd.items[0][0][:].rearrange("p a b -> p (a b)").bitcast(F32)
        f1 = self.wd.items[1][0][:].rearrange("p a b -> p (a b)").bitcast(F32)
        tl = [(f0[:, i * 512:(i + 1) * 512], Res()) for i in range(4)] + [(f1[:, i * 512:(i + 1) * 512], Res()) for i in range(4)]
        qn_r = Rot(tl[0:2])
        cs_r = Rot(tl[2:4])
        sn_r = Rot(tl[4:6])
        t1, r_t1 = tl[6]
        t2, r_t2 = tl[7]
        ab = self.aT[:].rearrange("p a b -> p (a b)")
        ost = Rot([(ab[:, i * 512:(i + 1) * 512], Res(), "ost%d" % i) for i in range(2 * TB // 512)])
        gcol = [self.sm[:, 48:49], self.sm[:, 49:50]]

        def dst(ci, t, ps, r_ps):
            isq = ci < 16
            gtok = tok0 + t * 512
            sq, r_sq = self.sq.next()
            P.op("act", lambda e: e.activation(out=sq[:], in_=ps[:], func=AF.Square), reads=[r_ps], writes=[r_sq])
            pn, r_pn = self.rotB.next()
            P.op("pe", lambda e: e.matmul(pn[:], self.ones_bf[:], sq[:], start=True, stop=True), reads=[r_sq, self.r_ones], writes=[r_pn])
            P.op("act", lambda e: e.activation(out=self.rstd[:], in_=pn[:], func=AF.Sqrt, scale=1.0 / 128.0, bias=self.eps_ap()),
                 reads=[r_pn, self.r_eps], writes=[self.r_rstd])
            P.op("dve", lambda e: e.reciprocal(out=self.rstd[:], in_=self.rstd[:]), reads=[self.r_rstd], writes=[self.r_rstd])
            o, r_o, ko = ost.next()
            dstap = (self.QT[ci] if isq else self.KT[ci - 16])[:, gtok:gtok + 512]
            if gtok < 512 or getattr(self, 'skip_rope', False):
                P.op("dve", lambda e: e.scalar_tensor_tensor(out=o, in0=ps[:], scalar=gcol[0 if isq else 1], in1=self.rstd[:], op0=ALU.mult, op1=ALU.mult),
                     reads=[r_ps, self.r_sm, self.r_rstd], writes=[r_o])
            else:
                pos0 = gtok - 512
                qn, r_qn = qn_r.next()
                P.op("dve", lambda e: e.scalar_tensor_tensor(out=qn, in0=ps[:], scalar=gcol[0 if isq else 1], in1=self.rstd[:], op0=ALU.mult, op1=ALU.mult),
                     reads=[r_ps, self.r_sm, self.r_rstd], writes=[r_qn])
                cs, r_cs = cs_r.next()
                sn, r_sn = sn_r.next()
                P.dma("sp", lambda e: e.dma_start(out=cs, in_=self.ropeT[0, :, pos0:pos0 + 512]), writes=[r_cs], semkey=self.lrkey(cs))
                P.dma("sp", lambda e: e.dma_start(out=sn, in_=self.ropeT[1, :, pos0:pos0 + 512]), writes=[r_sn], semkey=self.lrkey(sn))
                pr, r_pr = self.rotB.next()
                P.op("pe", lambda e: e.matmul(pr[:], self.ropeR, qn, start=True, stop=True), reads=[r_qn, self.r_cst], writes=[r_pr])
                P.op("dve", lambda e: e.tensor_tensor(out=t1, in0=qn, in1=cs, op=ALU.mult), reads=[r_qn, r_cs], writes=[r_t1])
                P.op("dve", lambda e: e.tensor_tensor(out=t2, in0=pr[:], in1=sn, op=ALU.mult), reads=[r_pr, r_sn], writes=[r_t2])
                P.op("dve", lambda e: e.tensor_tensor(out=o, in0=t1, in1=t2, op=ALU.add), reads=[r_t1, r_t2], writes=[r_o])
            P.dma("sp", lambda e: e.dma_start(out=dstap, in_=o), reads=[r_o], semkey=ko)
        if not getattr(self, 'skip_qk', False):
            self.proj_fm(wv, 0, 2560, dst)

        def dstv(tg, ps, r_ps):
            gtok = tok0 + tg * 128
            o, r_o, ko = ost.next()
            if gtok < 512:
                qn, r_qn = qn_r.next()
                P.op("dve", lambda e: e.tensor_copy(qn, ps[:]), reads=[r_ps], writes=[r_qn])
                P.op("act", lambda e: e.activation(out=o, in_=qn, func=AF.Copy), reads=[r_qn], writes=[r_o])
                P.dma("sp", lambda e: e.dma_start(out=self.o_v[gtok:gtok + 128, :], in_=qn), reads=[r_qn], semkey=self.lrkey(qn))
            else:
                P.op("act", lambda e: e.activation(out=o, in_=ps[:], func=AF.Copy), reads=[r_ps], writes=[r_o])
            P.dma("sp", lambda e: e.dma_start(out=self.VT[gtok:gtok + 128, :], in_=o), reads=[r_o], semkey=ko)
        if not getattr(self, 'skip_v', False):
            self.proj_tm(wv, 2560, dstv)
        if blk == 0 and not getattr(self, 'skip_k', False):
            kgb, r_kgb = tl[6]
            kr = self.qkg_row[1:2, :]
            src_b = bass.AP(kr.tensor, kr.offset, [[0, 128], [1, 128]])
            P.dma("sp", lambda e: e.dma_start(out=kgb[:, 0:128], in_=src_b), writes=[r_kgb], semkey=self.lrkey(kgb))
            ssk, r_ssk = tl[7][0][:, 0:8], tl[7][1]

            def dstk(tg, ps, r_ps):
                gtok = tok0 + tg * 128
                junk, r_junk = cs_r.next()
                for h in range(4):
                    P.op("act", lambda e, h=h: e.activation(out=junk[:, 0:128], in_=ps[:, h * 128:(h + 1) * 128], func=AF.Square, accum_out=ssk[:, h:h + 1]),
                         reads=[r_ps], writes=[r_junk, r_ssk])
                P.op("act", lambda e: e.activation(out=ssk[:, 0:4], in_=ssk[:, 0:4], func=AF.Sqrt, scale=1.0 / 128.0, bias=self.eps_ap()),
                     reads=[r_ssk, self.r_eps], writes=[r_ssk])
                P.op("dve", lambda e: e.reciprocal(out=ssk[:, 0:4], in_=ssk[:, 0:4]), reads=[r_ssk], writes=[r_ssk])
                qn, r_qn = qn_r.next()
                for h in range(4):
                    P.op("dve", lambda e, h=h: e.scalar_tensor_tensor(out=qn[:, h * 128:(h + 1) * 128], in0=ps[:, h * 128:(h + 1) * 128], scalar=ssk[:, h:h + 1],
                                                                      in1=kgb[:, 0:128], op0=ALU.mult, op1=ALU.mult), reads=[r_ps, r_ssk, r_kgb], writes=[r_qn])
                P.dma("sp", lambda e: e.dma_start(out=self.o_k[gtok:gtok + 128, :], in_=qn), reads=[r_qn], semkey=self.lrkey(qn))
            self.proj_tm(wv, 2048, dstk, tgs=range(4))
        P.barrier()

    def attn_core(self, segs):
        P = self.P
        P.barrier()
        self.arena_reset()
        NKM = max(sg[1] + (256 if sg[2][0] == 's' else 0) for sg in segs)
        KTa, r_KTa = self.t16(NKM)
        Va, r_Va = self.t16(NKM)
        qrot = Rot([self.t16(512) for _ in range(2)])
        prot = Rot([self.t16(512) for _ in range(4)])
        orot = Rot([self.t16(512) for _ in range(2)])
        rsrot = Rot([self.t32(512) for _ in range(2)])
        negb, r_negb = self.t32(4)
        gr, r_gr = self.t32(256, parts=1)
        gm, r_gm = self.t32(4, parts=1)
        P.dma("sp", lambda e: e.dma_start(out=gr, in_=self.qkg_row.rearrange("(o a) b -> o (a b)", o=1)), writes=[r_gr], semkey=self.key("c"))
        P.op("dve", lambda e: e.tensor_reduce(out=gm[:, 0:2], in_=gr.rearrange("p (a b) -> p a b", a=2), axis=AX.X, op=ALU.max, apply_absolute_value=True),
             reads=[r_gr], writes=[r_gm])
        P.op("dve", lambda e: e.tensor_tensor(out=gm[:, 2:3], in0=gm[:, 0:1], in1=gm[:, 1:2], op=ALU.mult), reads=[r_gm], writes=[r_gm])
        P.op("dve", lambda e: e.tensor_scalar(out=gm[:, 2:3], in0=gm[:, 2:3], scalar1=-(128.0 ** 0.5), scalar2=None, op0=ALU.mult), reads=[r_gm], writes=[r_gm])
        pb, r_pb = self.psM
        P.op("pe", lambda e: e.matmul(pb[:, 0:1], self.onesF[0:1, :], gm[:, 2:3], start=True, stop=True), reads=[r_gm, self.r_cst], writes=[r_pb])
        P.op("dve", lambda e: e.tensor_copy(negb[:, 0:1], pb[:, 0:1]), reads=[r_pb], writes=[r_negb])
        psS = Rot(self.psA)
        psP = Rot([(self.psB[0], self.psB[1]), (self.psB[2], self.psM)])
        scale = 128.0 ** -0.5

        def seg_body(tok0, S, kind):
            is_s = kind[0] == 's'
            off = 256 if is_s else 0
            NK = S + off
            NKT = NK // 128
            QB = 512 if S >= 512 else S

            def kv_body(kvh):
                if is_s:
                    P.dma("pool", lambda e: e.dma_start(out=KTa[:, 0:256], in_=self.cacheKT[kvh]), writes=[r_KTa], semkey="kta")
                    P.dma("pool", lambda e: e.dma_start(out=Va[:, 0:256].rearrange("p (c d) -> p c d", d=128),
                                                        in_=self.cacheV[:, kvh, :].rearrange("(c p) d -> p c d", p=128)), writes=[r_Va], semkey="va")
                for s0 in range(0, S, 1024):
                    s1 = min(S, s0 + 1024)
                    P.dma("sp", lambda e, s0=s0, s1=s1: e.dma_start(out=KTa[:, off + s0:off + s1], in_=self.KT[kvh][:, tok0 + s0:tok0 + s1]), writes=[r_KTa], semkey="kta2")
                    P.dma("sp", lambda e, s0=s0, s1=s1: e.dma_start(out=Va[:, off + s0:off + s1].rearrange("p (c d) -> p c d", d=128),
                                                                  in_=self.VT[tok0 + s0:tok0 + s1, kvh * 128:(kvh + 1) * 128].rearrange("(c p) d -> p c d", p=128)),
                          writes=[r_Va], semkey="va2")

                def q_body(head, qb):
                    q0 = tok0 + qb * QB
                    qT, r_qT = qrot.next()
                    P.dma("sp", lambda e: e.dma_start(out=qT[:, 0:QB], in_=self.QT[head][:, q0:q0 + QB]), writes=[r_qT], semkey=self.lrkey(qT))
                    (po, r_po), (pz, r_pz) = psP.next()
                    pend = []

                    def pv(kt, pt):
                        ptile, r_pt = pt

                        def f(e):
                            e.matmul(po[:, 0:QB], Va[:, kt * 128:(kt + 1) * 128], ptile[:, 0:QB], start=(kt == 0), stop=(kt == NKT - 1))
                            return e.matmul(pz[:, 0:QB], self.ones_bf[:], ptile[:, 0:QB], start=(kt == 0), stop=(kt == NKT - 1))
                        P.op("pe", f, reads=[r_Va, r_pt, self.r_ones], writes=[r_po, r_pz])

                    def sc(kt):
                        ps, r_ps = psS.next()
                        P.op("pe", lambda e: e.matmul(ps[:, 0:QB], KTa[:, kt * 128:(kt + 1) * 128], qT[:, 0:QB], start=True, stop=True),
                             reads=[r_KTa, r_qT], writes=[r_ps])
                        pt = prot.next()
                        P.op("act", lambda e: e.activation(out=pt[0][:, 0:QB], in_=ps[:, 0:QB], func=AF.Exp, scale=scale, bias=negb[:, 0:1]),
                             reads=[r_ps, r_negb], writes=[pt[1]])
                        return pt
                    for kt in range(NKT):
                        pend.append((kt, sc(kt)))
                        if len(pend) > 2:
                            pv(*pend.pop(0))
                    while pend:
                        pv(*pend.pop(0))
                    rs, r_rs = rsrot.next()
                    P.op("dve", lambda e: e.reciprocal(out=rs[:, 0:QB], in_=pz[:, 0:QB]), reads=[r_pz], writes=[r_rs])
                    o, r_o = orot.next()
                    P.op("dve", lambda e: e.tensor_tensor(out=o[:, 0:QB], in0=po[:, 0:QB], in1=rs[:, 0:QB], op=ALU.mult), reads=[r_po, r_rs], writes=[r_o])
                    P.dma("pool", lambda e: e.dma_start(out=self.OT[head][:, q0:q0 + QB], in_=o[:, 0:QB]), reads=[r_o], semkey=self.lrkey(o))
                for g in range(4):
                    for qb in range(S // QB):
                        q_body(kvh * 4 + g, qb)
            for kvh in range(4):
                kv_body(kvh)
        for sg in segs:
            seg_body(*sg)
        P.barrier()

    def build(self):
        self.declare_io()
        self.declare_io2()
        self.alloc_eps()
        self.alloc_common()
        self.alloc_consts()
        NB = self.NBLK
        self.modulation(0)
        for blk in range(NB):
            self.load_x(blk, self.xT_in)
            self.adaln(blk, 0, 0)
            self.ffn(blk, 0, 0)
            self.adaln(blk, 0, 1)
            self.inproj(blk)
            self.store_x(blk, self.XS)
        self.mixer_core(self.segs)
        self.modulation(1)
        for blk in range(NB):
            self.load_x(blk, self.XS)
            self.outproj(blk, self.w_out[0], self.YT, 0)
            self.adaln(blk, 0, 2)
            self.ffn(blk, 0, 1)
            self.adaln(blk, 1, 0)
            self.ffn(blk, 1, 0)
            self.adaln(blk, 1, 1)
            self.qkvproj(blk)
            self.store_x(blk, self.XS)
        self.attn_core(self.segs)
        for blk in range(NB):
            self.load_x(blk, self.XS)
            self.outproj(blk, self.w_o[0], self.OT, 1)
            self.adaln(blk, 1, 2)
            self.ffn(blk, 1, 1)
            self.store_x(blk, self.xT_out)
        self.P.emit()
        self.stack.close()
        return self.nc

    def build_test_attn(self):
        self.declare_io()
        self.declare_io2()
        self.alloc_eps()
        self.alloc_common()
        self.alloc_consts()
        self.modulation(1)
        for blk in range(self.NBLK):
            self.load_x(blk, self.xT_in)
            self.adaln(blk, 1, 1)
            self.qkvproj(blk)
        if not getattr(self, 'skip_core', False):
            self.attn_core(self.segs)
        self.dbgY = self.dram_out("dbgY", [16, 128, self.T], BF16)
        stg, r_stg = self.hT[:].rearrange("p a b -> p (a b)"), Res()
        for ci in range(16):
            self.P.dma("sp", lambda e, ci=ci: e.dma_start(out=stg[:, 0:self.T], in_=(self.QT if getattr(self, 'skip_core', False) else self.OT)[ci]), writes=[r_stg], semkey="dbg1")
            self.P.dma("sp", lambda e, ci=ci: e.dma_start(out=self.dbgY[ci], in_=stg[:, 0:self.T]), reads=[r_stg], semkey="dbg2")
        self.P.emit()
        self.stack.close()
        return self.nc


def host_common(inp):
    f32 = np.float32
    g = lambda k: np.asarray(inp[k], f32)
    cm = {}
    cm["w_mod"] = g("w_mod")
    cm["b_modT"] = np.ascontiguousarray(g("b_mod").reshape(2, NMOD * KC, 128).transpose(0, 2, 1))
    cm["norm_gT"] = np.ascontiguousarray(g("norm_g").reshape(2 * 3 * KC, 128).T)
    cm["ffn_w_gate"] = g("ffn_w_gate")
    cm["ffn_w_up"] = g("ffn_w_up")
    cm["ffn_w_down"] = g("ffn_w_down")
    cm["w_in_ab"] = g("w_in_ab")
    cm["w_out_ab"] = g("w_out_ab")
    cm["w_qkv"] = g("w_qkv")
    cm["w_o"] = g("w_o")
    consts = np.zeros((128, 768), f32)
    consts[:, 0:128] = np.eye(128, dtype=f32)
    ii = np.arange(128)
    consts[:, 128:256] = (ii[:, None] <= ii[None, :]).astype(f32)
    consts[:, 256:384] = (ii[:, None] >= ii[None, :]).astype(f32)
    consts[:, 384:512] = 1.0
    RT = np.zeros((128, 128), f32)
    for d in range(128):
        j = (d % 64) // 32
        k = d + 32 if j == 0 else d - 32
        RT[k, d] = 1.0
    consts[:, 512:640] = RT
    cm["consts"] = consts
    cw = g("mlstm_conv_w")[0]
    cb = g("mlstm_conv_b")[0]
    convT = np.zeros((128, 4, 8), f32)
    for tap in range(3):
        convT[:, tap, :] = cw[tap].reshape(8, 128).T
    convT[:, 3, :] = cb.reshape(8, 128).T
    cm["convT"] = convT
    bi = g("mlstm_b_i")[0]
    bf = g("mlstm_b_f")[0]
    cm["mbias"] = np.ascontiguousarray(np.stack([bi[0], bf[0], bi[1], bf[1]], axis=1))
    og = np.concatenate([g("mlstm_out_g")[0], g("gla_out_g")[0]])
    cm["outgT"] = np.ascontiguousarray(og.reshape(16, 128).T)
    gkwb = np.zeros((17, 2, 512), f32)
    gkwb[0:16] = g("gla_w_gk")[0].transpose(1, 0, 2)
    gkwb[16] = g("gla_b_gk")[0]
    cm["gkwb"] = gkwb
    cm["qkg"] = np.ascontiguousarray(np.stack([g("q_norm_g")[0], g("k_norm_g")[0]], axis=1))
    cm["qkg_row"] = np.ascontiguousarray(np.stack([g("q_norm_g")[0], g("k_norm_g")[0]], axis=0))
    t = np.arange(4096)
    row = (t // 64).astype(f32)
    col = (t % 64).astype(f32)
    inv = (10000.0 ** (-np.arange(0, 64, 2, dtype=f32) / 64.0)).astype(f32)
    cosT = np.zeros((128, 4096), f32)
    sinT = np.zeros((128, 4096), f32)
    for d in range(128):
        a = d // 64
        j = (d % 64) // 32
        i = d % 32
        ang = (row if a == 0 else col) * inv[i]
        cosT[d] = np.cos(ang)
        sinT[d] = np.sin(ang) * (-1.0 if j == 0 else 1.0)
    cm["ropeT"] = np.stack([cosT, sinT], 0)
    return cm


def host_core(inp, cm, c, bsel):
    f32 = np.float32
    g = lambda k: np.asarray(inp[k], f32)
    d = dict(cm)
    d["st_C"] = np.ascontiguousarray(g("state_mlstm_C")[bsel, 0])
    d["st_n"] = np.ascontiguousarray(g("state_mlstm_n")[bsel, 0])
    d["st_m"] = np.ascontiguousarray(g("state_mlstm_m")[bsel, 0])
    d["st_S"] = np.ascontiguousarray(g("state_gla_S")[bsel, 0])
    d["cacheKT"] = np.ascontiguousarray(g("cache_k")[bsel, 0].transpose(1, 2, 0))
    d["cacheV"] = np.ascontiguousarray(g("cache_v")[bsel, 0])
    cT = np.stack([g("c_ctx"), g("c")[bsel]], -1).reshape(KC, 128, 2).transpose(1, 0, 2)
    d["cT"] = np.ascontiguousarray(cT)
    return d


_IN_NAMES = ["xT_in", "cT", "w_mod", "b_modT", "norm_gT", "ffn_w_gate", "ffn_w_up", "ffn_w_down", "w_in_ab", "w_out_ab", "w_qkv", "w_o",
             "consts", "convT", "mbias", "outgT", "gkwb", "qkg", "qkg_row", "st_C", "st_n", "st_m", "st_S", "cacheKT", "cacheV", "ropeT"]
_CACHE = {}


def _get_nc():
    if "nc" not in _CACHE:
        b = Builder(1536, 3)
        b.segs = [(0, 256, ('p', 0)), (256, 256, ('p', 1)), (512, 4096, ('s',))]
        _CACHE["nc"] = b.build()
    return _CACHE["nc"]


def kernel(**inp):
    nc = _get_nc()
    f32 = np.float32
    cm = host_common(inp)
    xp = np.asarray(inp["x_prompt"], f32)
    xs = np.asarray(inp["x_sample"], f32)
    in_maps = []
    for c in range(8):
        bsel = c % 2
        d = host_core(inp, cm, c, bsel)
        x = np.concatenate([xp[2 * c:2 * c + 2].reshape(512, D), xs[bsel]], 0)
        d["xT_in"] = np.ascontiguousarray(x.T)
        in_maps.append({k: d[k] for k in _IN_NAMES})
    res = run_bass_kernel_spmd(nc, in_maps, core_ids=list(range(8)))
    R = res.results
    y_prompt = np.zeros((16, 256, D), f32)
    y_sample = np.zeros((2, 4096, D), f32)
    oC = np.zeros((16, 1, 2, 4, 128, 256), f32)
    on = np.zeros((16, 1, 2, 4, 128), f32)
    om = np.zeros((16, 1, 2, 4), f32)
    oS = np.zeros((16, 1, 2, 4, 128, 256), f32)
    ok = np.zeros((16, 1, 256, 4, 128), f32)
    ov = np.zeros((16, 1, 256, 4, 128), f32)
    for c in range(8):
        y = np.asarray(R[c]["xT_out"]).T
        y_prompt[2 * c:2 * c + 2] = y[:512].reshape(2, 256, D)
        if c < 2:
            y_sample[c] = y[512:]
        oC[2 * c:2 * c + 2, 0] = np.asarray(R[c]["o_C"])
        on[2 * c:2 * c + 2, 0] = np.asarray(R[c]["o_n"])
        om[2 * c:2 * c + 2, 0] = np.asarray(R[c]["o_m"])
        oS[2 * c:2 * c + 2, 0] = np.asarray(R[c]["o_S"])
        ok[2 * c:2 * c + 2, 0] = np.asarray(R[c]["o_k"]).reshape(2, 256, 4, 128)
        ov[2 * c:2 * c + 2, 0] = np.asarray(R[c]["o_v"]).reshape(2, 256, 4, 128)
    return (y_prompt, y_sample, oC, on, om, oS, ok, ov)
```
